# Optimizing a Trainium2 kernel written in Bass

```python
import jax, jax.numpy as jnp
from jax import lax
import numpy as np

D_MODEL = 2048
BATCH = 4
SEQ = 8192
DEPTH = 1

GLA_HEADS = 4
GLA_DK = 256
GLA_DV = 512
GLA_GATE_RANK = 16
GLA_TAU = 16.0
GLA_CHUNK = 64
GLA_QK = GLA_HEADS * GLA_DK
GLA_VW = GLA_HEADS * GLA_DV

DIL_PATTERNS = ((128, 1), (512, 4), (2048, 16))
DIL_GROUPS = len(DIL_PATTERNS)
DIL_HEADS = 8
DIL_HEAD_DIM = 128
DIL_BLOCK = 128
DIL_W = DIL_GROUPS * DIL_HEADS * DIL_HEAD_DIM
DIL_OUT = DIL_HEADS * DIL_HEAD_DIM
ROPE_THETA = 10000.0

IN_SPLITS = (GLA_QK, GLA_QK, GLA_VW, GLA_VW, GLA_GATE_RANK, DIL_W, DIL_W, DIL_W, D_MODEL, D_MODEL)
IN_WIDTH = 2 * GLA_QK + 2 * GLA_VW + GLA_GATE_RANK + 3 * DIL_W + 2 * D_MODEL
VALUE_SEGMENTS = (2, 7)

D_FF = 5632
N_MOD = 9
LN_EPS = 1e-5
DN_ALPHA = (2.0 * DEPTH) ** 0.25
DN_BETA = (8.0 * DEPTH) ** -0.25

kernel_name = "hybrid_gla_dilated_macaron_deepnorm_adaln"


def layer_norm(x, g, b):
    xf = x.astype(jnp.float32)
    mu = jnp.mean(xf, -1, keepdims=True)
    var = jnp.mean(jnp.square(xf - mu), -1, keepdims=True)
    return ((xf - mu) * lax.rsqrt(var + LN_EPS) * g + b).astype(x.dtype)


def modulate(x, shift, scale):
    return x * (1.0 + scale[:, None, :]) + shift[:, None, :]


def swiglu(h, w_gu, w_down):
    gate, up = jnp.split(h @ w_gu, 2, axis=-1)
    return (jax.nn.silu(gate) * up) @ w_down


def rope(t, positions):
    half = t.shape[-1] // 2
    freq = ROPE_THETA ** (-jnp.arange(half, dtype=jnp.float32) / half)
    ang = positions.astype(jnp.float32)[..., None] * freq
    cos = jnp.cos(ang)[:, :, None, :]
    sin = jnp.sin(ang)[:, :, None, :]
    t1, t2 = t[..., :half].astype(jnp.float32), t[..., half:].astype(jnp.float32)
    return jnp.concatenate([t1 * cos - t2 * sin, t2 * cos + t1 * sin], -1).astype(t.dtype)


def gla_chunked(q, k, v, log_a):
    B, S, H, Dk = q.shape
    Dv = v.shape[-1]
    C = GLA_CHUNK
    nc = S // C

    def to_chunks(t):
        return t.astype(jnp.float32).reshape(B, nc, C, H, -1).transpose(1, 0, 3, 2, 4)

    qc, kc, vc, ac = to_chunks(q), to_chunks(k), to_chunks(v), to_chunks(log_a)
    b = jnp.cumsum(ac, axis=3)
    b_last = b[:, :, :, -1:, :]
    q_dec = qc * jnp.exp(b) * (Dk ** -0.5)
    k_inv = kc * jnp.exp(-b)
    k_end = kc * jnp.exp(b_last - b)
    causal = jnp.tril(jnp.ones((C, C), dtype=bool))
    a_intra = jnp.where(causal, jnp.einsum('nbhcd,nbhsd->nbhcs', q_dec, k_inv), 0.0)
    o_intra = jnp.einsum('nbhcs,nbhse->nbhce', a_intra, vc)
    decay = jnp.exp(b_last[:, :, :, 0, :])

    def step(state, inp):
        q_i, k_i, v_i, d_i = inp
        o_i = jnp.einsum('bhcd,bhde->bhce', q_i, state)
        state = d_i[..., None] * state + jnp.einsum('bhcd,bhce->bhde', k_i, v_i)
        return state, o_i

    state0 = jnp.zeros((B, H, Dk, Dv), jnp.float32)
    _, o_inter = lax.scan(step, state0, (q_dec, k_end, vc, decay))
    return (o_intra + o_inter).transpose(1, 0, 3, 2, 4).reshape(B, S, H, Dv)


def dilated_window_attention(q, k, v, window, dilation):
    B, S, H, Dh = q.shape
    r = dilation
    L = S // r
    W = window // r
    nb = -(-L // DIL_BLOCK)
    Lp = nb * DIL_BLOCK

    def strided_blocks(t):
        t = t.reshape(B, L, r, H, Dh).transpose(0, 2, 1, 3, 4)
        t = jnp.pad(t, ((0, 0), (0, 0), (0, Lp - L), (0, 0), (0, 0)))
        return t.reshape(B, r, nb, DIL_BLOCK, H, Dh)

    def with_previous_block(t):
        prev = jnp.pad(t, ((0, 0), (0, 0), (1, 0), (0, 0), (0, 0), (0, 0)))[:, :, :-1]
        return jnp.concatenate([prev, t], axis=3)

    qb = strided_blocks(q)
    kk = with_previous_block(strided_blocks(k))
    vv = with_previous_block(strided_blocks(v))
    s = jnp.einsum('brnqhd,brnkhd->brnhqk', qb, kk).astype(jnp.float32) * (Dh ** -0.5)
    qi = jnp.arange(DIL_BLOCK)[:, None]
    kj = jnp.arange(2 * DIL_BLOCK)[None, :]
    rel = qi + DIL_BLOCK - kj
    band = (rel >= 0) & (rel <= W)
    after_start = (jnp.arange(nb)[:, None, None] > 0) | (kj >= DIL_BLOCK)[None]
    mask = band[None] & after_start
    s = jnp.where(mask[None, None, :, None], s, -jnp.inf)
    m = jnp.max(s, -1, keepdims=True)
    p = jnp.exp(s - m)
    den = jnp.sum(p, -1, keepdims=True)
    o = jnp.einsum('brnhqk,brnkhd->brnhqd', p, vv.astype(jnp.float32)) / den
    lse = (m + jnp.log(den))[..., 0]
    o = o.transpose(0, 1, 2, 4, 3, 5).reshape(B, r, Lp, H, Dh)[:, :, :L]
    o = o.transpose(0, 2, 1, 3, 4).reshape(B, S, H, Dh)
    lse = lse.transpose(0, 1, 2, 4, 3).reshape(B, r, Lp, H)[:, :, :L]
    lse = lse.transpose(0, 2, 1, 3).reshape(B, S, H)
    return o, lse


def hybrid_mixer(h, positions, w_in, w_alpha2, b_alpha, gla_norm_g, w_branch_a, w_branch_b, w_out):
    B, S, _ = h.shape
    split_points = [int(p) for p in np.cumsum(IN_SPLITS)[:-1]]
    (gq, gk, gv, gr, g_lr, dq, dk, dv, gate_a, gate_b) = jnp.split(h @ w_in, split_points, axis=-1)

    gate_logits = (g_lr @ w_alpha2 + b_alpha).astype(jnp.float32)
    log_a = jax.nn.log_sigmoid(gate_logits) / GLA_TAU
    o_a = gla_chunked(gq.reshape(B, S, GLA_HEADS, GLA_DK),
                      gk.reshape(B, S, GLA_HEADS, GLA_DK),
                      gv.reshape(B, S, GLA_HEADS, GLA_DV),
                      log_a.reshape(B, S, GLA_HEADS, GLA_DK))
    o_a = o_a * lax.rsqrt(jnp.mean(jnp.square(o_a), -1, keepdims=True) + LN_EPS) * gla_norm_g
    o_a = o_a.reshape(B, S, GLA_VW).astype(h.dtype) * jax.nn.silu(gr)
    y_a = o_a @ w_branch_a

    dq = rope(dq.reshape(B, S, DIL_GROUPS * DIL_HEADS, DIL_HEAD_DIM), positions)
    dk = rope(dk.reshape(B, S, DIL_GROUPS * DIL_HEADS, DIL_HEAD_DIM), positions)
    dq = dq.reshape(B, S, DIL_GROUPS, DIL_HEADS, DIL_HEAD_DIM)
    dk = dk.reshape(B, S, DIL_GROUPS, DIL_HEADS, DIL_HEAD_DIM)
    dv = dv.reshape(B, S, DIL_GROUPS, DIL_HEADS, DIL_HEAD_DIM)
    outs, lses = [], []
    for g, (window, dilation) in enumerate(DIL_PATTERNS):
        o_g, lse_g = dilated_window_attention(dq[:, :, g], dk[:, :, g], dv[:, :, g], window, dilation)
        outs.append(o_g)
        lses.append(lse_g)
    wts = jax.nn.softmax(jnp.stack(lses, 0), axis=0)
    o_b = jnp.sum(wts[..., None] * jnp.stack(outs, 0), axis=0)
    y_b = o_b.reshape(B, S, DIL_OUT).astype(h.dtype) @ w_branch_b

    merged = jax.nn.sigmoid(gate_a) * y_a + jax.nn.sigmoid(gate_b) * y_b
    return merged @ w_out


def setup_inputs(seed: int = 0) -> dict:
    key = jax.random.key(seed)
    ks = jax.random.split(key, 24)
    f32 = jnp.float32
    nrm = lambda k, shape, s: jax.random.normal(k, shape, f32) * s
    col_scale = jnp.concatenate([jnp.full((w,), DN_BETA if i in VALUE_SEGMENTS else 1.0, f32)
                                 for i, w in enumerate(IN_SPLITS)])
    return {
        "x": nrm(ks[0], (BATCH, SEQ, D_MODEL), 1.0),
        "c": nrm(ks[1], (BATCH, D_MODEL), 1.0),
        "positions": jnp.broadcast_to(jnp.arange(SEQ, dtype=jnp.int32), (BATCH, SEQ)),
        "w_ada": nrm(ks[2], (DEPTH, D_MODEL, N_MOD * D_MODEL), D_MODEL ** -0.5),
        "b_ada": nrm(ks[3], (DEPTH, N_MOD * D_MODEL), 0.02),
        "ln1_g": 1.0 + nrm(ks[4], (DEPTH, D_MODEL), 0.02),
        "ln1_b": nrm(ks[5], (DEPTH, D_MODEL), 0.02),
        "w_ffn1_gu": nrm(ks[6], (DEPTH, D_MODEL, 2 * D_FF), D_MODEL ** -0.5),
        "w_ffn1_down": nrm(ks[7], (DEPTH, D_FF, D_MODEL), DN_BETA * D_FF ** -0.5),
        "w_in": nrm(ks[8], (DEPTH, D_MODEL, IN_WIDTH), D_MODEL ** -0.5) * col_scale,
        "w_alpha2": nrm(ks[9], (DEPTH, GLA_GATE_RANK, GLA_QK), GLA_GATE_RANK ** -0.5),
        "b_alpha": nrm(ks[10], (DEPTH, GLA_QK), 0.1),
        "gla_norm_g": 1.0 + nrm(ks[11], (DEPTH, GLA_HEADS, GLA_DV), 0.02),
        "w_branch_a": nrm(ks[12], (DEPTH, GLA_VW, D_MODEL), GLA_VW ** -0.5),
        "w_branch_b": nrm(ks[13], (DEPTH, DIL_OUT, D_MODEL), DIL_OUT ** -0.5),
        "w_out": nrm(ks[14], (DEPTH, D_MODEL, D_MODEL), DN_BETA * D_MODEL ** -0.5),
        "ln2_g": 1.0 + nrm(ks[15], (DEPTH, D_MODEL), 0.02),
        "ln2_b": nrm(ks[16], (DEPTH, D_MODEL), 0.02),
        "w_ffn2_gu": nrm(ks[17], (DEPTH, D_MODEL, 2 * D_FF), D_MODEL ** -0.5),
        "w_ffn2_down": nrm(ks[18], (DEPTH, D_FF, D_MODEL), DN_BETA * D_FF ** -0.5),
        "ln3_g": 1.0 + nrm(ks[19], (DEPTH, D_MODEL), 0.02),
        "ln3_b": nrm(ks[20], (DEPTH, D_MODEL), 0.02),
    }


def reference(x, c, positions, w_ada, b_ada, ln1_g, ln1_b, w_ffn1_gu, w_ffn1_down,
              w_in, w_alpha2, b_alpha, gla_norm_g, w_branch_a, w_branch_b, w_out,
              ln2_g, ln2_b, w_ffn2_gu, w_ffn2_down, ln3_g, ln3_b):
    c_act = jax.nn.silu(c)
    for l in range(DEPTH):
        mods = c_act @ w_ada[l] + b_ada[l]
        sh1, sc1, g1, sh2, sc2, g2, sh3, sc3, g3 = jnp.split(mods, N_MOD, axis=-1)
        f1 = swiglu(modulate(x, sh1, sc1), w_ffn1_gu[l], w_ffn1_down[l])
        x = layer_norm(DN_ALPHA * x + 0.5 * g1[:, None, :] * f1, ln1_g[l], ln1_b[l])
        m = hybrid_mixer(modulate(x, sh2, sc2), positions, w_in[l], w_alpha2[l], b_alpha[l],
                         gla_norm_g[l], w_branch_a[l], w_branch_b[l], w_out[l])
        x = layer_norm(DN_ALPHA * x + g2[:, None, :] * m, ln2_g[l], ln2_b[l])
        f2 = swiglu(modulate(x, sh3, sc3), w_ffn2_gu[l], w_ffn2_down[l])
        x = layer_norm(DN_ALPHA * x + 0.5 * g3[:, None, :] * f2, ln3_g[l], ln3_b[l])
    return x
```

```python
import math
import numpy as np
from contextlib import ExitStack
import concourse.bass as bass
import concourse.mybir as mybir
from concourse.bass_utils import run_bass_kernel_spmd

F32 = mybir.dt.float32
BF16 = mybir.dt.bfloat16
I32 = mybir.dt.int32
AF = mybir.ActivationFunctionType
ALU = mybir.AluOpType

D = 2048
KC = 16
SEQ = 8192
BATCH = 4
NTOK = 4096
T = 512
NT = NTOK // T
DFF = 5632
FC = DFF // 128
INW = 19472
LN_EPS = 1e-5
ALPHA = 2.0 ** 0.25
CQ, CK, CV, CR, CDQ, CDK, CDV, CGA, CGB = 0, 8, 16, 32, 48, 72, 96, 120, 136
NCH = 152
WIN_BLOCK_COLS = [c for c in range(0, 6144, 512)] + [c for c in range(6160, INW, 512)]
assert len(WIN_BLOCK_COLS) == 38
DIL = (1, 4, 16)

ENGS = ("pe", "act", "dve", "pool", "sp")
DMA_K = 8


class Op:
    __slots__ = ("eng", "fn", "deps", "signal", "count", "sem", "is_dma", "pre_wait", "idx")

    def __init__(self, eng, fn, is_dma=False):
        self.eng = eng
        self.fn = fn
        self.deps = []
        self.signal = False
        self.count = 0
        self.sem = None
        self.is_dma = is_dma
        self.pre_wait = None
        self.idx = 0


class Tok:
    __slots__ = ("w", "r")

    def __init__(self):
        self.w = None
        self.r = []


class Sched:
    def __init__(self, dma_queues=("sp", "pool")):
        self.prog = {e: [] for e in ENGS}
        self.dma_queues = dma_queues
        self.last = {e: None for e in ENGS}

    def _add(self, op, reads, writes):
        best = {}
        dmas = {}

        def add(d):
            if d is None:
                return
            if d.is_dma:
                dmas[id(d)] = d
            else:
                b = best.get(d.eng)
                if b is None or d.idx > b.idx:
                    best[d.eng] = d

        for t in reads:
            add(t.w)
        for t in writes:
            add(t.w)
            for r in t.r:
                add(r)
        op.deps = list(best.values()) + list(dmas.values())
        op.idx = len(self.prog[op.eng])
        for t in reads:
            if (not op.is_dma) and t.r and (not t.r[-1].is_dma) and t.r[-1].eng == op.eng:
                t.r[-1] = op
            else:
                t.r.append(op)
        for t in writes:
            t.w = op
            t.r = []
        self.prog[op.eng].append(op)
        if op.fn is not None:
            self.last[op.eng] = op
        return op

    def op(self, eng, fn, reads=(), writes=()):
        return self._add(Op(eng, fn), reads, writes)

    def dma(self, q, fn, reads=(), writes=()):
        return self._add(Op(q, fn, is_dma=True), reads, writes)

    def barrier(self):
        lasts = [self.last[e] for e in ENGS if self.last[e] is not None and not self.last[e].is_dma]
        for q in self.dma_queues:
            lasts.extend([o for o in self.prog[q] if o.is_dma][-DMA_K:])
        for e in ENGS:
            comp = [o for o in self.prog[e][-4000:] if (not o.is_dma) and o.fn is not None]
            if comp and comp[-1] not in lasts:
                lasts.append(comp[-1])
        for e in ENGS:
            op = Op(e, None)
            op.deps = list(lasts)
            op.idx = len(self.prog[e])
            self.prog[e].append(op)

    def emit(self, sems):
        for e in ENGS:
            for op in self.prog[e]:
                for d in op.deps:
                    if d.is_dma:
                        continue
                    if d.eng == "pe" and op.eng == "pe" and not op.is_dma and op.fn is not None:
                        continue
                    d.signal = True
        for e in ENGS:
            c = 0
            dn = 0
            for op in self.prog[e]:
                if op.is_dma:
                    k = dn % DMA_K
                    op.sem = sems[("dma", e, k)]
                    need = 16 * (dn // DMA_K)
                    op.pre_wait = (op.sem, need) if need > 0 else None
                    op.count = need + 16
                    dn += 1
                elif op.signal:
                    c += 1
                    op.count = c
                    op.sem = sems[e]
        out = {}
        for e in ENGS:
            waited = {}
            lst = []
            for op in self.prog[e]:
                for d in op.deps:
                    if (not d.is_dma) and d.eng == "pe" and e == "pe" and (not op.is_dma) and op.fn is not None:
                        continue
                    key = id(d.sem)
                    if waited.get(key, 0) >= d.count:
                        continue
                    waited[key] = d.count
                    lst.append(("w", d.sem, d.count))
                if op.is_dma and op.pre_wait is not None:
                    key = id(op.pre_wait[0])
                    if waited.get(key, 0) < op.pre_wait[1]:
                        waited[key] = op.pre_wait[1]
                        lst.append(("w", op.pre_wait[0], op.pre_wait[1]))
                if op.fn is not None:
                    lst.append(("o", op))
            out[e] = lst
        return out


def emit_all(nc, sched, es):
    sems = {}
    for e in ENGS:
        sems[e] = es.enter_context(nc.semaphore(f"s_{e}"))
    for q in sched.dma_queues:
        for k in range(DMA_K):
            sems[("dma", q, k)] = es.enter_context(nc.semaphore(f"d_{q}_{k}"))
    lists = sched.emit(sems)

    def run(eng, lst):
        for it in lst:
            if it[0] == "w":
                eng.wait_ge(it[1], it[2])
            else:
                op = it[1]
                ins = op.fn(eng)
                if op.is_dma:
                    ins.then_inc(op.sem, 16)
                elif op.signal:
                    ins.then_inc(op.sem, 1)

    with nc.Block() as block:
        @block.tensor
        def _(e):
            run(e, lists["pe"])

        @block.scalar
        def _(e):
            run(e, lists["act"])

        @block.vector
        def _(e):
            run(e, lists["dve"])

        @block.gpsimd
        def _(e):
            run(e, lists["pool"])

        @block.sync
        def _(e):
            run(e, lists["sp"])


def build_nc():
    import os
    KSUB = int(os.environ.get("KSUB", "99"))
    KDBG = int(os.environ.get("KDBG", "99"))
    KDUMP = int(os.environ.get("KDUMP", "0"))
    nc = bass.Bass("TRN2", target_bir_lowering=False)
    dt_in = lambda name, shape, dt=F32: nc.dram_tensor(name, shape, dt, kind="ExternalInput").ap()
    x_d = dt_in("x", [NTOK, D])
    c_d = dt_in("c_l", [128, KC])
    pos_d = dt_in("pos", [1, NTOK], I32)
    wada_d = dt_in("w_ada", [D, 9 * D])
    bada_d = dt_in("b_ada_l", [128, 144])
    wgu_d = [dt_in("w_ffn1_gu", [D, 2 * DFF]), dt_in("w_ffn2_gu", [D, 2 * DFF])]
    wdn_d = [dt_in("w_ffn1_down", [DFF, D]), dt_in("w_ffn2_down", [DFF, D])]
    win_d = dt_in("w_in", [D, INW])
    wal_d = dt_in("w_alpha_aug", [32, 1024])
    wba_d = dt_in("w_branch_a", [D, D])
    wbb_d = dt_in("w_branch_b", [1024, D])
    wout_d = dt_in("w_out", [D, D])
    vecs_d = dt_in("vecs", [128, 7, KC])
    cst_d = dt_in("consts", [128, 8, 128])
    msk_d = dt_in("masks", [128, 2, 256])
    out_d = nc.dram_tensor("out", [NTOK, D], F32, kind="ExternalOutput").ap()

    dram = lambda name, shape, dt: nc.dram_tensor(name, shape, dt).ap()
    wgu_s = [dram("wgu1_s", [22, 128, KC, 512], BF16), dram("wgu2_s", [22, 128, KC, 512], BF16)]
    wdn_s = [dram("wdn1_s", [16, 128, FC, 128], BF16), dram("wdn2_s", [16, 128, FC, 128], BF16)]
    win_s = dram("win_s", [38, 128, KC, 512], BF16)
    wba_s = dram("wba_s", [4, 128, KC, 512], BF16)
    wbb_s = dram("wbb_s", [4, 128, 8, 512], BF16)
    wout_s = dram("wout_s", [4, 128, KC, 512], BF16)
    projT = dram("projT", [NCH * 128, NTOK], BF16)
    glrT = dram("glrT", [16, NTOK], F32)
    x1a = dram("x1a", [D, NTOK], F32)
    oaT = dram("oaT", [D, NTOK], BF16)
    obT = dram("obT", [1024, NTOK], BF16)
    cosT = dram("cosT", [128, NTOK], F32)
    sinT = dram("sinT", [128, NTOK], F32)
    XROWS = 8192 + 2048 + 8192 + 32768
    XIN = nc.dram_tensor("xin", [XROWS, 128], BF16)
    XOUT = nc.dram_tensor("xout", [8 * XROWS, 128], BF16)
    XLOC = nc.dram_tensor("xloc", [XROWS, 128], BF16)
    XOFF = {"st": (0, 8192, 8), 0: (8192, 2048, 1), 1: (10240, 8192, 4), 2: (18432, 32768, 16)}

    def xview(t, key, base=0):
        o, n, b = XOFF[key]
        ap = t.ap()[base + o: base + o + n, :]
        return ap.rearrange("(a b) c -> a (b c)", b=b) if b > 1 else ap

    slot_cache = {}

    def xview_dyn(e, key, row0=0, nrows=None):
        o, n, b = XOFF[key]
        if nrows is None:
            nrows = n // b
        if id(e) not in slot_cache:
            slot_cache[id(e)] = e.snap(((e.partition_id() // 2) * 2) * XROWS)
        base = slot_cache[id(e)]
        ap = XOUT.ap()[bass.ds(base, XROWS), :][o + row0 * b: o + row0 * b + nrows * b, :]
        return ap.rearrange("(a b) c -> a (b c)", b=b) if b > 1 else ap

    with ExitStack() as es:
        sbt = lambda name, shape, dt: es.enter_context(nc.sbuf_tensor(name, shape, dt))
        R = sbt("R", [128, 8192], F32)
        H = sbt("H", [128, 8192], BF16)
        AT = sbt("AT", [128, 22528], BF16)
        WR = sbt("WR", [128, 3 * 8192], BF16)
        SQ = sbt("SQ", [128, 1024], F32)
        ST3 = sbt("ST3", [128, 3 * 512], F32)
        TMPF = sbt("TMPF", [128, 1024], F32)
        STG = sbt("STG", [128, 3 * 2048], BF16)
        CS = sbt("CS", [128, 2 * 1024], F32)
        TSB = sbt("TSB", [128, 1024], F32)
        CST = sbt("CST", [128, 8, 128], F32)
        IDB = sbt("IDB", [128, 128], BF16)
        ONB = sbt("ONB", [128, 128], BF16)
        MSK = sbt("MSK", [128, 2, 256], F32)
        VEC = sbt("VEC", [128, 7, KC], F32)
        PV = sbt("PV", [128, 16, KC], F32)
        MODS = sbt("MODS", [128, 144], F32)
        BADA = sbt("BADA", [128, 144], F32)
        CACT = sbt("CACT", [128, KC], F32)
        GLW = sbt("GLW", [128, KC, 16], F32)
        WAL = sbt("WAL", [32, 1024], F32)
        GLRS = sbt("GLRS", [32, 512], F32)
        GLWB = sbt("GLWB", [128, KC, 16], BF16)
        GLST = sbt("GLST", [32, 512], F32)
        SM = sbt("SM", [128, 256], F32)
        PB = [es.enter_context(nc.psum_tensor(f"pb{i}", [128, 512], F32)) for i in range(8)]

        ATf = AT[:, 0:22528].bitcast(F32)
        IDENT, PERM, ONESM, USUF, UTRI, MASKA = (CST[:, i, :] for i in range(6))
        CIND = CST[:, 6, 0:2]
        FREQ = CST[:, 6, 2:3]
        SGN = CST[:, 6, 3:4]
        FLAG = CST[:, 6, 4:5]
        EPS = CST[:, 6, 5:6]
        ZERO = CST[:, 6, 6:7]
        HALFPI = CST[:, 6, 7:8]

        S = Sched()
        tR = [Tok() for _ in range(KC)]
        tH = [Tok() for _ in range(KC)]
        tAT = [Tok() for _ in range(FC)]
        tWR = [Tok() for _ in range(3)]
        tPB = [Tok() for _ in range(8)]
        tSTG = [Tok() for _ in range(3)]
        tSQ = [Tok(), Tok()]
        tST = [Tok(), Tok(), Tok()]
        tTMP = [Tok(), Tok()]
        tCS = [Tok(), Tok()]
        tTSB = [Tok(), Tok()]
        tC = Tok()
        tPV = Tok()
        tSM = Tok()
        tGLRS = Tok()
        tGLST = Tok()

        cnt = {"wr": 0, "stg": 0}

        def ld(out, in_, writes, reads=(), q="sp"):
            return S.dma(q, lambda e: e.dma_start(out=out, in_=in_), reads=reads, writes=writes)

        def st(out, in_, reads, writes=(), q="pool"):
            return S.dma(q, lambda e: e.dma_start(out=out, in_=in_), reads=reads, writes=writes)

        def mm(out, lhsT, rhs, start, stop, reads, writes):
            return S.op("pe", lambda e: e.matmul(out, lhsT=lhsT, rhs=rhs, start=start, stop=stop), reads, writes)

        def tr(out, in_, ident, reads, writes):
            return S.op("pe", lambda e: e.transpose(out, in_, ident), reads, writes)

        def act(out, in_, func, reads, writes, bias=None, scale=1.0):
            if bias is None:
                return S.op("act", lambda e: e.activation(out=out, in_=in_, func=func, scale=scale), reads, writes)
            return S.op("act", lambda e: e.activation(out=out, in_=in_, func=func, bias=bias, scale=scale), reads,
                        writes)

        def tt(out, in0, in1, op, reads, writes, eng="dve"):
            return S.op(eng, lambda e: e.tensor_tensor(out=out, in0=in0, in1=in1, op=op), reads, writes)

        def ts(out, in0, s1, s2, op0, op1, reads, writes, eng="dve"):
            return S.op(eng, lambda e: e.tensor_scalar(out=out, in0=in0, scalar1=s1, scalar2=s2, op0=op0, op1=op1),
                        reads, writes)

        def tss(out, in0, s1, op0, reads, writes, eng="dve"):
            return S.op(eng, lambda e: e.tensor_single_scalar(out=out, in_=in0, scalar=s1, op=op0), reads, writes)

        def stt(out, in0, scalar, in1, op0, op1, reads, writes, eng="dve"):
            return S.op(eng, lambda e: e.scalar_tensor_tensor(out=out, in0=in0, scalar=scalar, in1=in1, op0=op0,
                                                              op1=op1), reads, writes)

        def cp(out, in_, reads, writes, eng="dve"):
            if eng == "act":
                return S.op("act", lambda e: e.copy(out=out, in_=in_), reads, writes)
            return S.op(eng, lambda e: e.tensor_copy(out=out, in_=in_), reads, writes)

        def Rk(kc, a=0, b=T):
            return R[:, kc * T + a: kc * T + b]

        def Hk(kc, a=0, b=T):
            return H[:, kc * T + a: kc * T + b]

        def ATk(j, a=0, b=T):
            return AT[:, j * T + a: j * T + b]

        def WRs(s):
            return WR[:, s * 8192:(s + 1) * 8192]

        def pv(i):
            return PV[:, i, :]

        (P_A1, P_B1, P_HG1, P_L1RA, P_L1RB, P_L1HA, P_L1HB, P_G2, P_L2RA, P_L2RB, P_L2HA, P_L2HB, P_HG3, P_L3RA,
         P_L3RB, P_GG) = range(16)

        def phase0():
            ld(CST[:], cst_d, [tC])
            ld(MSK[:], msk_d, [tC])
            ld(VEC[:], vecs_d, [tC])
            ld(BADA[:], bada_d, [tC])
            ld(CACT[:], c_d, [tC])
            ld(GLW[:], win_d[:, 6144:6160].rearrange("(kc p) n -> p kc n", p=128), [tC])
            ld(WAL[:], wal_d, [tC])
            cp(GLWB[:], GLW[:], [tC], [tC])
            cp(IDB[:], IDENT, [tC], [tC])
            S.op("dve", lambda e: e.memset(ONB[:], 1.0), [], [tC])
            S.op("dve", lambda e: e.memset(GLRS[:], 1.0), [], [tGLRS])
            act(CACT[:], CACT[:], AF.Silu, [tC], [tC])
            stg = [(R[:, 0:8192], tR), (ATf[:, 0:8192], tAT)]
            for jb in range(36):
                buf, toks = stg[jb % 2]
                ld(buf.rearrange("p (kc n) -> p kc n", kc=KC),
                   wada_d[:, jb * 512:(jb + 1) * 512].rearrange("(kc p) n -> p kc n", p=128), toks)
                for mc in range(4):
                    col = jb * 4 + mc
                    for kc in range(KC):
                        mm(PB[0][:, col:col + 1], buf[:, kc * 512 + mc * 128: kc * 512 + (mc + 1) * 128],
                           CACT[:, kc:kc + 1], kc == 0, kc == KC - 1, list(toks) + [tC], [tPB[0]])
            tt(MODS[:], PB[0][:, 0:144], BADA[:], ALU.add, [tPB[0], tC], [tPV])
            md = lambda m: MODS[:, m * 16:(m + 1) * 16]
            g = lambda i: VEC[:, i, :]
            rw = ([tPV, tC], [tPV])
            ts(pv(P_A1), md(1), 1.0, 1.0 / ALPHA, ALU.add, ALU.mult, *rw)
            cp(pv(P_B1), md(0), *rw)
            tss(pv(P_HG1), md(2), 0.5, ALU.mult, *rw)
            tss(pv(P_L1RA), g(0), ALPHA, ALU.mult, *rw)
            tss(pv(P_L1RB), g(1), ALPHA, ALU.mult, *rw)
            tss(pv(P_GG), md(4), 1.0, ALU.add, *rw)
            tt(pv(P_L1HA), g(0), pv(P_GG), ALU.mult, *rw)
            tt(pv(P_L1HB), g(1), pv(P_GG), ALU.mult, *rw)
            tt(pv(P_L1HB), pv(P_L1HB), md(3), ALU.add, *rw)
            cp(pv(P_G2), md(5), *rw)
            tss(pv(P_L2RA), g(2), ALPHA, ALU.mult, *rw)
            tss(pv(P_L2RB), g(3), ALPHA, ALU.mult, *rw)
            tss(pv(P_GG), md(7), 1.0, ALU.add, *rw)
            tt(pv(P_L2HA), g(2), pv(P_GG), ALU.mult, *rw)
            tt(pv(P_L2HB), g(3), pv(P_GG), ALU.mult, *rw)
            tt(pv(P_L2HB), pv(P_L2HB), md(6), ALU.add, *rw)
            tss(pv(P_HG3), md(8), 0.5, ALU.mult, *rw)
            cp(pv(P_L3RA), g(4), *rw)
            cp(pv(P_L3RB), g(5), *rw)
            cp(pv(P_GG), g(6), *rw)
            S.barrier()
            C1 = 6.28125
            C2 = 2.0 * math.pi - C1
            PIC = 3.1415925
            posI = H[:, 0:8192].bitcast(I32)
            tpos = Tok()
            ld(posI[:, 0:NTOK], pos_d.partition_broadcast(128), [tpos], q="pool")
            tl = [Tok() for _ in range(6)]
            for pc in range(4):
                sl = slice(pc * 1024, (pc + 1) * 1024)
                posf = R[:, 0:1024]
                cp(posf, posI[:, sl], [tpos], [tl[0]])
                for which in range(2):
                    t0 = R[:, 1024:2048]
                    tmp = R[:, 2048:3072]
                    nI = R[:, 3072:4096].bitcast(I32)
                    nF = R[:, 4096:5120]
                    r = R[:, 5120:6144]
                    ts(t0, posf, FREQ, HALFPI if which == 0 else ZERO, ALU.mult, ALU.add, [tl[0], tC], [tl[1]])
                    tss(tmp, t0, 1.0 / (2.0 * math.pi), ALU.mult, [tl[1]], [tl[2]])
                    cp(nI, tmp, [tl[2]], [tl[3]])
                    cp(nF, nI, [tl[3]], [tl[4]])
                    stt(r, nF, -C1, t0, ALU.mult, ALU.add, [tl[4], tl[1]], [tl[5]])
                    stt(r, nF, -C2, r, ALU.mult, ALU.add, [tl[4], tl[5]], [tl[5]])
                    ts(r, r, -PIC, PIC, ALU.max, ALU.min, [tl[5]], [tl[5]])
                    if which == 0:
                        act(r, r, AF.Sin, [tl[5]], [tl[5]])
                    else:
                        act(r, r, AF.Sin, [tl[5], tC], [tl[5]], scale=SGN)
                    st((cosT if which == 0 else sinT)[:, sl], r, [tl[5]])
            S.barrier()
            sin_ = [(R[:, 0:8192], Tok()), (ATf[:, 0:8192], Tok())]
            sout = [(H[:, 0:8192], Tok()), (WRs(0), Tok())]
            n = [0]

            def conv(src_ap, dst_ap, kcn, nb):
                i = n[0] % 2
                n[0] += 1
                (bi, ti), (bo, to) = sin_[i], sout[i]
                sz = kcn * nb
                ld(bi[:, 0:sz].rearrange("p (kc n) -> p kc n", kc=kcn),
                   src_ap.rearrange("(kc p) n -> p kc n", p=128), [ti])
                if i == 0:
                    cp(bo[:, 0:sz], bi[:, 0:sz], [ti], [to], eng="dve")
                else:
                    cp(bo[:, 0:sz], bi[:, 0:sz], [ti], [to], eng="act")
                st(dst_ap, bo[:, 0:sz].rearrange("p (kc n) -> p kc n", kc=kcn), [to])

            for l in range(2):
                for b in range(22):
                    conv(wgu_d[l][:, b * 512:(b + 1) * 512], wgu_s[l][b], KC, 512)
                for b in range(16):
                    conv(wdn_d[l][:, b * 128:(b + 1) * 128], wdn_s[l][b], FC, 128)
            for b in range(38):
                c0 = WIN_BLOCK_COLS[b]
                conv(win_d[:, c0:c0 + 512], win_s[b], KC, 512)
            for b in range(4):
                conv(wba_d[:, b * 512:(b + 1) * 512], wba_s[b], KC, 512)
                conv(wbb_d[:, b * 512:(b + 1) * 512], wbb_s[b], 8, 512)
                conv(wout_d[:, b * 512:(b + 1) * 512], wout_s[b], KC, 512)

        def wload(src_block_ap, kcn, nb):
            s = cnt["wr"] % 3
            cnt["wr"] += 1
            ld(WRs(s)[:, 0:kcn * nb].rearrange("p (kc n) -> p kc n", kc=kcn), src_block_ap, [tWR[s]])
            return s

        def layernorm(ra, rb, ha, hb):
            mean_ps, ex2_ps = PB[6], PB[7]
            for kc in range(KC):
                mm(mean_ps[:, :], ONESM, Rk(kc), kc == 0, kc == KC - 1, [tR[kc], tC], [tPB[6]])
            for kc in range(KC):
                b = kc % 2
                act(SQ[:, b * 512:(b + 1) * 512], Rk(kc), AF.Square, [tR[kc]], [tSQ[b]])
                mm(ex2_ps[:, :], ONESM, SQ[:, b * 512:(b + 1) * 512], kc == 0, kc == KC - 1, [tSQ[b], tC], [tPB[7]])
            mean, sd, rstd = ST3[:, 0:512], ST3[:, 512:1024], ST3[:, 1024:1536]
            cp(mean, mean_ps[:, :], [tPB[6]], [tST[0]])
            tt(sd, mean, mean, ALU.mult, [tST[0]], [tST[1]])
            tt(sd, ex2_ps[:, :], sd, ALU.subtract, [tPB[7], tST[1]], [tST[1]])
            act(sd, sd, AF.Sqrt, [tST[1], tC], [tST[1]], bias=EPS)
            S.op("dve", lambda e: e.reciprocal(out=rstd, in_=sd), [tST[1]], [tST[2]])
            for kc in range(KC):
                b = kc % 2
                tmp = TMPF[:, b * 512:(b + 1) * 512]
                tt(tmp, Rk(kc), mean, ALU.subtract, [tR[kc], tST[0]], [tTMP[b]])
                tt(tmp, tmp, rstd, ALU.mult, [tTMP[b], tST[2]], [tTMP[b]])
                act(Rk(kc), tmp, AF.Identity, [tTMP[b], tPV], [tR[kc]], bias=pv(rb)[:, kc:kc + 1],
                    scale=pv(ra)[:, kc:kc + 1])
                if ha is not None:
                    ts(Hk(kc), tmp, pv(ha)[:, kc:kc + 1], pv(hb)[:, kc:kc + 1], ALU.mult, ALU.add,
                       [tTMP[b], tPV], [tH[kc]])

        def ffn(l, hg):
            for j4 in range(11):
                sg = wload(wgu_s[l][j4], KC, 512)
                su = wload(wgu_s[l][11 + j4], KC, 512)
                for c in range(4):
                    j = j4 * 4 + c
                    bg, bu = (0, 1) if c % 2 == 0 else (2, 3)
                    for kc in range(KC):
                        mm(PB[bg][:, :], WRs(sg)[:, kc * 512 + c * 128: kc * 512 + (c + 1) * 128], Hk(kc), kc == 0,
                           kc == KC - 1, [tWR[sg], tH[kc]], [tPB[bg]])
                    for kc in range(KC):
                        mm(PB[bu][:, :], WRs(su)[:, kc * 512 + c * 128: kc * 512 + (c + 1) * 128], Hk(kc), kc == 0,
                           kc == KC - 1, [tWR[su], tH[kc]], [tPB[bu]])
                    b = j % 2
                    tmp = TMPF[:, b * 512:(b + 1) * 512]
                    act(tmp, PB[bg][:, :], AF.Silu, [tPB[bg]], [tTMP[b]])
                    tt(ATk(j), tmp, PB[bu][:, :], ALU.mult, [tTMP[b], tPB[bu]], [tAT[j]])
            for m in range(KC):
                s = wload(wdn_s[l][m], FC, 128)
                bk = 4 + (m % 2)
                for j in range(FC):
                    mm(PB[bk][:, :], WRs(s)[:, j * 128:(j + 1) * 128], ATk(j), j == 0, j == FC - 1,
                       [tWR[s], tAT[j]], [tPB[bk]])
                stt(Rk(m), PB[bk][:, :], pv(hg)[:, m:m + 1], Rk(m), ALU.mult, ALU.add, [tPB[bk], tPV, tR[m]],
                    [tR[m]])

        def phaseA():
            for it in range(NT):
                t0 = it * T
                cb = it % 2
                csb = CS[:, cb * 1024:(cb + 1) * 1024]
                ld(csb[:, 0:512], cosT[:, t0:t0 + T], [tCS[cb]])
                ld(csb[:, 512:1024], sinT[:, t0:t0 + T], [tCS[cb]])
                xs = [ATf[:, b * 2048:(b + 1) * 2048] for b in range(4)]
                for b in range(4):
                    ld(xs[b], x_d[t0 + b * 128: t0 + (b + 1) * 128, :], tAT[b * 8:(b + 1) * 8])
                for kc in range(KC):
                    bk = kc % 4
                    if KDBG < 2:
                        continue
                    for b in range(4):
                        tr(PB[bk][:, b * 128:(b + 1) * 128], xs[b][:, kc * 128:(kc + 1) * 128], IDENT,
                           tAT[b * 8:(b + 1) * 8] + [tC], [tPB[bk]])
                    act(Rk(kc), PB[bk][:, :], AF.Identity, [tPB[bk]], [tR[kc]], scale=ALPHA)
                    ts(Hk(kc), Rk(kc), pv(P_A1)[:, kc:kc + 1], pv(P_B1)[:, kc:kc + 1], ALU.mult, ALU.add,
                       [tR[kc], tPV], [tH[kc]])
                if KSUB < 2:
                    continue
                ffn(0, P_HG1)
                if KSUB < 3:
                    continue
                if KDUMP == 1:
                    dump_R(t0)
                    continue
                layernorm(P_L1RA, P_L1RB, P_L1HA, P_L1HB)
                if KDUMP == 2:
                    dump_R(t0)
                    continue
                if KSUB < 4:
                    continue
                st(x1a[:, t0:t0 + T].rearrange("(kc p) t -> p kc t", p=128),
                   R[:, 0:8192].rearrange("p (kc t) -> p kc t", kc=KC), list(tR))
                if KSUB < 5:
                    continue
                for kc in range(KC):
                    mm(PB[5][0:16, :], GLWB[:, kc, :], Hk(kc), kc == 0, kc == KC - 1, [tC, tH[kc]], [tPB[5]])
                cp(GLST[0:16, :], PB[5][0:16, :], [tPB[5]], [tGLST])
                st(glrT[:, t0:t0 + T], GLST[0:16, :], [tGLST])
                ch = 0
                for b in range(38):
                    s = wload(win_s[b], KC, 512)
                    sg = cnt["stg"] % 3
                    cnt["stg"] += 1
                    stg = STG[:, sg * 2048:(sg + 1) * 2048]
                    for c in range(4):
                        bk = c % 4
                        for kc in range(KC):
                            mm(PB[bk][:, :], WRs(s)[:, kc * 512 + c * 128: kc * 512 + (c + 1) * 128], Hk(kc),
                               kc == 0, kc == KC - 1, [tWR[s], tH[kc]], [tPB[bk]])
                        o = stg[:, c * 512:(c + 1) * 512]
                        if CR <= ch < CDQ:
                            act(o, PB[bk][:, :], AF.Silu, [tPB[bk]], [tSTG[sg]])
                        elif ch >= CGA:
                            act(o, PB[bk][:, :], AF.Sigmoid, [tPB[bk]], [tSTG[sg]])
                        elif CDQ <= ch < CDV:
                            tb = ch % 2
                            tsb = TSB[:, tb * 512:(tb + 1) * 512]
                            cp(tsb, PB[bk][:, :], [tPB[bk]], [tTSB[tb]], eng="act")
                            mm(PB[4 + tb][:, :], PERM, tsb, True, True, [tC, tTSB[tb]], [tPB[4 + tb]])
                            tt(tsb, tsb, csb[:, 0:512], ALU.mult, [tTSB[tb], tCS[cb]], [tTSB[tb]])
                            tmp = TMPF[:, tb * 512:(tb + 1) * 512]
                            tt(tmp, PB[4 + tb][:, :], csb[:, 512:1024], ALU.mult, [tPB[4 + tb], tCS[cb]], [tTMP[tb]])
                            tt(o, tsb, tmp, ALU.add, [tTSB[tb], tTMP[tb]], [tSTG[sg]])
                        else:
                            if ch % 2 == 0:
                                cp(o, PB[bk][:, :], [tPB[bk]], [tSTG[sg]], eng="act")
                            else:
                                cp(o, PB[bk][:, :], [tPB[bk]], [tSTG[sg]], eng="dve")
                        ch += 1
                    st(projT[(ch - 4) * 128: ch * 128, t0:t0 + T].rearrange("(c p) t -> p c t", p=128),
                       stg.rearrange("p (c t) -> p c t", c=4), [tSTG[sg]])


        thx = [Tok() for _ in range(3)]

        def exchange_halo():
            for g in range(3):
                w = 128 * DIL[g]
                hin = xview(XIN, g)
                st(hin[0:1024, :], projT[(CDK + g * 8) * 128:(CDK + g * 8 + 8) * 128, NTOK - w:NTOK], [], [tst],
                   q="sp")
                st(hin[1024:2048, :], projT[(CDV + g * 8) * 128:(CDV + g * 8 + 8) * 128, NTOK - w:NTOK], [],
                   [tst], q="sp")

        def gla_pass(use_recv):
            GLR = R[0:32, 0:128]
            LG = R[:, 1024:2048]
            EX = R[:, 2048:3072]
            EX2 = R[:, 3072:4096]
            S32 = R[:, 4096:8192]
            EX3 = SQ[:, 0:1024]
            QD = AT[:, 0:1024]
            KI = AT[:, 1024:2048]
            KEND = AT[:, 2048:3072]
            VTOK = AT[:, 3072:5120]
            AM = AT[:, 5120:5632]
            SQB = AT[:, 5632:6144]
            OAB = AT[:, 6144:8192]
            SBF = AT[:, 8192:12288]
            GLB = [WR[:, 0:12288], WR[:, 12288:24576]]
            DEC = SM[:, 0:16]
            RST = SM[:, 128:256]
            tk = {k: Tok() for k in ["glr", "lg", "ex", "ex2", "ex3", "qd", "ki", "kend", "vtok", "am", "sqb", "oab",
                                     "dec", "rst", "glb0", "glb1", "tmp"]}
            tS32 = [Tok() for _ in range(8)]
            tSBF = [Tok() for _ in range(8)]
            tO = tPB[0:4]
            S.op("dve", lambda e: e.memset(R[0:32, 0:128], 1.0), [], [tk["glr"]])
            if use_recv:
                ld(S32.bitcast(BF16).rearrange("p (c n) -> p c n", c=8),
                   xview(XLOC, "st").rearrange("(c p) n -> p c n", p=128), tS32, reads=[tst])
                for c8 in range(8):
                    tss(S32[:, c8 * 512:(c8 + 1) * 512], S32[:, c8 * 512:(c8 + 1) * 512], FLAG, ALU.mult,
                        [tS32[c8], tC], [tS32[c8]])
            else:
                for c8 in range(8):
                    S.op("dve", lambda e, c8=c8: e.memset(S32[:, c8 * 512:(c8 + 1) * 512], 0.0), [], [tS32[c8]])
            for c8 in range(8):
                cp(SBF[:, c8 * 512:(c8 + 1) * 512], S32[:, c8 * 512:(c8 + 1) * 512], [tS32[c8]], [tSBF[c8]], eng="act")

            for bi in range(NTOK // 128):
                tb = bi * 128
                sb_i = (bi // 2) % 2
                glb = GLB[sb_i].rearrange("p (c t) -> p c t", c=48)
                tglb = tk[f"glb{sb_i}"]
                so_ = (bi % 2) * 128
                if bi % 2 == 0:
                    t2 = (bi // 2) * 256
                    ld(glb, projT[0:48 * 128, t2:t2 + 256].rearrange("(c p) t -> p c t", p=128), [tglb],
                       reads=[tproj])
                qv = lambda c: glb[:, CQ + c, so_:so_ + 128]
                kv = lambda c: glb[:, CK + c, so_:so_ + 128]
                vv = lambda c: glb[:, CV + c, so_:so_ + 128]
                rv = lambda c: glb[:, CR + c, so_:so_ + 128]
                ld(GLR[0:16, :], glrT[:, tb:tb + 128], [tk["glr"]], reads=[tproj])
                for n2 in range(2):
                    bk = 5 + n2
                    mm(PB[bk][:, :], R[0:17, 0:128], WAL[0:17, n2 * 512:(n2 + 1) * 512], True, True,
                       [tk["glr"], tC, tGLRS], [tPB[bk]])
                    act(EX[:, n2 * 512:(n2 + 1) * 512], PB[bk][:, :], AF.Exp, [tPB[bk]], [tk["ex"]], scale=-1.0)
                    act(LG[:, n2 * 512:(n2 + 1) * 512], EX[:, n2 * 512:(n2 + 1) * 512], AF.Ln, [tk["ex"]],
                        [tk["lg"]], bias=1.0)
                for n2 in range(2):
                    bk = 5 + n2
                    mm(PB[bk][:, :], USUF, LG[:, n2 * 512:(n2 + 1) * 512], True, True, [tC, tk["lg"]], [tPB[bk]])
                    act(EX[:, n2 * 512:(n2 + 1) * 512], PB[bk][:, :], AF.Exp, [tPB[bk]], [tk["ex"]])
                p7b = PB[7][:].bitcast(BF16)
                for c in range(8):
                    tr(p7b[:, c * 128:(c + 1) * 128], kv(c), IDB[:], [tglb, tC], [tPB[7]])
                tt(KEND, p7b[:, 0:1024], EX, ALU.mult, [tPB[7], tk["ex"]], [tk["kend"]])
                for hf in range(2):
                    pb = PB[5 + hf][:].bitcast(BF16)
                    for c in range(8):
                        tr(pb[:, c * 128:(c + 1) * 128], vv(hf * 8 + c), IDB[:], [tglb, tC], [tPB[5 + hf]])
                    cp(VTOK[:, hf * 1024:(hf + 1) * 1024], pb[:, 0:1024], [tPB[5 + hf]], [tk["vtok"]], eng="act")
                for hf in range(2):
                    bk = 5 + hf
                    for c in range(4):
                        cc_ = hf * 4 + c
                        mm(PB[bk][:, c * 128:(c + 1) * 128], LG[:, cc_ * 128:(cc_ + 1) * 128], UTRI, True, True,
                           [tk["lg"], tC], [tPB[bk]])
                    act(EX2[:, hf * 512:(hf + 1) * 512], PB[bk][:, :], AF.Exp, [tPB[bk]], [tk["ex2"]])
                    act(EX3[:, hf * 512:(hf + 1) * 512], PB[bk][:, :], AF.Exp, [tPB[bk]], [tk["ex3"]], scale=-1.0)
                for c in range(8):
                    stt(QD[:, c * 128:(c + 1) * 128], qv(c), 1.0 / 16.0, EX2[:, c * 128:(c + 1) * 128], ALU.mult,
                        ALU.mult, [tglb, tk["ex2"]], [tk["qd"]])
                    tt(KI[:, c * 128:(c + 1) * 128], kv(c), EX3[:, c * 128:(c + 1) * 128], ALU.mult,
                       [tglb, tk["ex3"]], [tk["ki"]])
                for c in range(8):
                    mm(PB[7][:, c * 2:c * 2 + 2], LG[:, c * 128:(c + 1) * 128], CIND, True, True, [tk["lg"], tC],
                       [tPB[7]])
                act(DEC, PB[7][:, 0:16], AF.Exp, [tPB[7]], [tk["dec"]])
                for h in range(4):
                    for dc in range(2):
                        c = h * 2 + dc
                        mm(PB[4][:, h * 128:(h + 1) * 128], KI[:, c * 128:(c + 1) * 128],
                           QD[:, c * 128:(c + 1) * 128], dc == 0, dc == 1, [tk["ki"], tk["qd"]], [tPB[4]])
                for h in range(4):
                    tt(AM[:, h * 128:(h + 1) * 128], PB[4][:, h * 128:(h + 1) * 128], MASKA, ALU.mult,
                       [tPB[4], tC], [tk["am"]])
                for h in range(4):
                    for ec in range(4):
                        mm(PB[h][:, ec * 128:(ec + 1) * 128], VTOK[:, (h * 4 + ec) * 128:(h * 4 + ec + 1) * 128],
                           AM[:, h * 128:(h + 1) * 128], ec == 0, False, [tk["vtok"], tk["am"]], [tO[h]])
                for j in range(2):
                    for h in range(4):
                        for ec in range(4):
                            for dc in range(2):
                                c = h * 2 + dc
                                mm(PB[h][:, ec * 128 + 64 * j: ec * 128 + 64 * j + 64],
                                   SBF[:, c * 512 + ec * 128: c * 512 + (ec + 1) * 128],
                                   QD[:, c * 128 + 64 * j: c * 128 + 64 * j + 64], False,
                                   (j == 1 and dc == 1), [tSBF[c], tk["qd"]], [tO[h]])
                    for h in range(4):
                        for dc in range(2):
                            c = h * 2 + dc
                            bk = 5 + (c % 2)
                            mm(PB[bk][:, :], KEND[64 * j:64 * j + 64, c * 128:(c + 1) * 128],
                               VTOK[64 * j:64 * j + 64, h * 512:(h + 1) * 512], True, True,
                               [tk["kend"], tk["vtok"]], [tPB[bk]])
                            stt(S32[:, c * 512:(c + 1) * 512], S32[:, c * 512:(c + 1) * 512],
                                DEC[:, c * 2 + j: c * 2 + j + 1], PB[bk][:, :], ALU.mult, ALU.add,
                                [tS32[c], tk["dec"], tPB[bk]], [tS32[c]])
                            cp(SBF[:, c * 512:(c + 1) * 512], S32[:, c * 512:(c + 1) * 512], [tS32[c]], [tSBF[c]],
                               eng="act")
                for h in range(4):
                    act(SQB, PB[h][:, :], AF.Square, [tO[h]], [tk["sqb"]])
                    for ec in range(4):
                        mm(PB[7][:, 0:128], ONB[:], SQB[:, ec * 128:(ec + 1) * 128], ec == 0, ec == 3,
                           [tC, tk["sqb"]], [tPB[7]])
                    act(RST, PB[7][:, 0:128], AF.Sqrt, [tPB[7], tC], [tk["rst"]], bias=EPS, scale=1.0 / 512.0)
                    S.op("dve", lambda e: e.reciprocal(out=RST, in_=RST), [tk["rst"]], [tk["rst"]])
                    for ec in range(4):
                        i = h * 4 + ec
                        tmp = TMPF[:, ec * 128:(ec + 1) * 128]
                        tt(tmp, PB[h][:, ec * 128:(ec + 1) * 128], RST, ALU.mult, [tO[h], tk["rst"]], [tk["tmp"]])
                        stt(OAB[:, i * 128:(i + 1) * 128], tmp, pv(P_GG)[:, i:i + 1], rv(i), ALU.mult, ALU.mult,
                            [tk["tmp"], tPV, tglb], [tk["oab"]])
                st(oaT[:, tb:tb + 128].rearrange("(c p) t -> p c t", p=128),
                   OAB.rearrange("p (c t) -> p c t", c=16), [tk["oab"]], [toa])
            st(xview(XIN, "st").rearrange("(c p) n -> p c n", p=128),
               S32.bitcast(BF16).rearrange("p (c n) -> p c n", c=8), tS32, [tst])

        tst = Tok()
        tproj = Tok()
        toa = Tok()

        def exchange_state():
            def cc(e):
                return e.collective_compute("AllGather", ALU.bypass, replica_groups=[list(range(8))],
                                            ins=[XIN.ap().opt()], outs=[XOUT.ap().opt()])
            S.op("pool", cc, [tst], [tst])

            def pick(e):
                base = e.snap(((e.partition_id() // 2) * 2) * XROWS)
                return e.dma_start(out=XLOC.ap(), in_=XOUT.ap()[bass.ds(base, XROWS), :])
            S.dma("sp", pick, reads=[tst], writes=[tst])

        def attention():
            OACC = R[:, 0:4096]
            DACC = R[:, 4096:8192]
            bufs = []
            for base, tot in ((WR, 0), (AT, 0)):
                bufs.append((base[:, 0:6144], base[:, 6144:12288], base[:, 12288:16384]))
            PEX = WR[:, 16384:16384 + 512].bitcast(F32)
            PMK = WR[:, 17408:17408 + 256]
            VTK = [WR[:, 18432:18560], WR[:, 18560:18688]]
            OBS = AT[:, 16384:20480]
            tb_ = [Tok(), Tok()]
            tpex, tpmk, tobs, tacc = Tok(), Tok(), Tok(), Tok()
            tvtk = [Tok(), Tok()]
            scale = 128.0 ** -0.5
            n_ld = 0
            for h in range(8):
                for g in range(3):
                    r = DIL[g]
                    w = 128 * r
                    KT, VT, QT = bufs[n_ld % 2]
                    tbuf = tb_[n_ld % 2]
                    n_ld += 1
                    ho = xview(XLOC, g)
                    ld(KT[:, 0:w], ho[h * 128:(h + 1) * 128, :], [tbuf], reads=[tst])
                    ld(VT[:, 0:w], ho[1024 + h * 128:1024 + (h + 1) * 128, :], [tbuf], reads=[tst])
                    ld(KT[:, w:w + NTOK], projT[(CDK + g * 8 + h) * 128:(CDK + g * 8 + h + 1) * 128, :], [tbuf],
                       reads=[tproj])
                    ld(VT[:, w:w + NTOK], projT[(CDV + g * 8 + h) * 128:(CDV + g * 8 + h + 1) * 128, :], [tbuf],
                       reads=[tproj])
                    ld(QT[:, 0:NTOK], projT[(CDQ + g * 8 + h) * 128:(CDQ + g * 8 + h + 1) * 128, :], [tbuf],
                       reads=[tproj])
                    nblk = NTOK // w
                    u = 0
                    for rho in range(r):
                        for n in range(nblk):
                            k0 = w * n + rho
                            q0 = w * n + rho
                            vprev, vcur = VTK[n % 2], VTK[(n + 1) % 2]
                            tvp, tvc = tvtk[n % 2], tvtk[(n + 1) % 2]
                            p7b = PB[7][:].bitcast(BF16)
                            if n == 0:
                                tr(p7b[:, 0:128], VT[:, k0: k0 + 127 * r + 1: r], IDB[:], [tbuf, tC], [tPB[7]])
                                cp(vprev, p7b[:, 0:128], [tPB[7]], [tvp], eng="act")
                            p6b = PB[6][:].bitcast(BF16)
                            tr(p6b[:, 0:128], VT[:, k0 + w: k0 + w + 127 * r + 1: r], IDB[:], [tbuf, tC], [tPB[6]])
                            cp(vcur, p6b[:, 0:128], [tPB[6]], [tvc], eng="act")
                            bs = u % 2
                            qs = QT[:, q0: q0 + 127 * r + 1: r]
                            mm(PB[bs][:, 0:128], KT[:, k0: k0 + 127 * r + 1: r], qs, True, True, [tbuf], [tPB[bs]])
                            mm(PB[bs][:, 128:256], KT[:, k0 + w: k0 + w + 127 * r + 1: r], qs, True, True, [tbuf],
                               [tPB[bs]])
                            act(PEX, PB[bs][:, 0:256], AF.Exp, [tPB[bs]], [tpex], scale=scale)
                            tt(PMK, PEX, MSK[:, 0 if n == 0 else 1, :], ALU.mult, [tpex, tC], [tpmk])
                            bo = 2 + (u % 2)
                            mm(PB[bo][:, 0:128], vprev, PMK[:, 0:128], True, False, [tvp, tpmk], [tPB[bo]])
                            mm(PB[bo][:, 0:128], vcur, PMK[:, 128:256], False, True, [tvc, tpmk], [tPB[bo]])
                            mm(PB[bo][:, 128:256], ONB[:], PMK[:, 0:128], True, False, [tC, tpmk], [tPB[bo]])
                            mm(PB[bo][:, 128:256], ONB[:], PMK[:, 128:256], False, True, [tC, tpmk], [tPB[bo]])
                            oa = OACC[:, q0: q0 + 127 * r + 1: r]
                            da = DACC[:, q0: q0 + 127 * r + 1: r]
                            if g == 0:
                                cp(oa, PB[bo][:, 0:128], [tPB[bo]], [tacc], eng="dve")
                                cp(da, PB[bo][:, 128:256], [tPB[bo]], [tacc], eng="dve")
                            else:
                                tt(oa, oa, PB[bo][:, 0:128], ALU.add, [tacc, tPB[bo]], [tacc])
                                tt(da, da, PB[bo][:, 128:256], ALU.add, [tacc, tPB[bo]], [tacc])
                            u += 1
                S.op("dve", lambda e: e.reciprocal(out=DACC, in_=DACC), [tacc], [tacc])
                tt(OBS, OACC, DACC, ALU.mult, [tacc], [tobs])
                st(obT[h * 128:(h + 1) * 128, :], OBS, [tobs], [tob])

        tob = Tok()

        def dump_R(t0):
            for b in range(4):
                osb = ATf[:, (b % 2) * 2048:((b % 2) + 1) * 2048]
                tos = tAT[(b % 2) * 8:((b % 2) + 1) * 8]
                for k4 in range(4):
                    bk = k4 % 4
                    for q in range(4):
                        kc = k4 * 4 + q
                        tr(PB[bk][:, q * 128:(q + 1) * 128], Rk(kc, b * 128, (b + 1) * 128), IDENT,
                           [tR[kc], tC], [tPB[bk]])
                    if k4 % 2 == 0:
                        cp(osb[:, k4 * 512:(k4 + 1) * 512], PB[bk][:, :], [tPB[bk]], tos, eng="act")
                    else:
                        cp(osb[:, k4 * 512:(k4 + 1) * 512], PB[bk][:, :], [tPB[bk]], tos, eng="dve")
                st(out_d[t0 + b * 128: t0 + (b + 1) * 128, :], osb, tos)

        def phaseC():
            for it in range(NT):
                t0 = it * T
                OA = AT[:, 0:8192]
                OB = AT[:, 8192:12288]
                SG = [AT[:, 12288 + i * 512: 12288 + (i + 1) * 512] for i in range(4)]
                tOA, tOB = tAT[0:16], tAT[16:24]
                tSG = tAT[24:28]
                ld(OA.rearrange("p (c t) -> p c t", c=16), oaT[:, t0:t0 + T].rearrange("(c p) t -> p c t", p=128),
                   tOA, reads=[toa])
                ld(OB.rearrange("p (c t) -> p c t", c=8), obT[:, t0:t0 + T].rearrange("(c p) t -> p c t", p=128),
                   tOB, reads=[tob])
                ld(R[:, 0:8192].rearrange("p (kc t) -> p kc t", kc=KC),
                   x1a[:, t0:t0 + T].rearrange("(kc p) t -> p kc t", p=128), list(tR))
                for nb in range(4):
                    sa = wload(wba_s[nb], KC, 512)
                    sb_ = wload(wbb_s[nb], 8, 512)
                    for c in range(4):
                        n = nb * 4 + c
                        ba, bb = (0, 1) if c % 2 == 0 else (2, 3)
                        for kc in range(KC):
                            mm(PB[ba][:, :], WRs(sa)[:, kc * 512 + c * 128: kc * 512 + (c + 1) * 128],
                               OA[:, kc * 512:(kc + 1) * 512], kc == 0, kc == KC - 1, [tWR[sa], tOA[kc]], [tPB[ba]])
                        for kc in range(8):
                            mm(PB[bb][:, :], WRs(sb_)[:, kc * 512 + c * 128: kc * 512 + (c + 1) * 128],
                               OB[:, kc * 512:(kc + 1) * 512], kc == 0, kc == 7, [tWR[sb_], tOB[kc]], [tPB[bb]])
                        i2 = (c % 2) * 2
                        ld(SG[i2], projT[(CGA + n) * 128:(CGA + n + 1) * 128, t0:t0 + T], [tSG[i2]], reads=[tproj])
                        ld(SG[i2 + 1], projT[(CGB + n) * 128:(CGB + n + 1) * 128, t0:t0 + T], [tSG[i2 + 1]],
                           reads=[tproj])
                        b = c % 2
                        tmp = TMPF[:, b * 512:(b + 1) * 512]
                        tsb = TSB[:, b * 512:(b + 1) * 512]
                        tt(tmp, PB[ba][:, :], SG[i2], ALU.mult, [tPB[ba], tSG[i2]], [tTMP[b]])
                        tt(tsb, PB[bb][:, :], SG[i2 + 1], ALU.mult, [tPB[bb], tSG[i2 + 1]], [tTSB[b]])
                        tt(Hk(n), tmp, tsb, ALU.add, [tTMP[b], tTSB[b]], [tH[n]])
                for nb in range(4):
                    so = wload(wout_s[nb], KC, 512)
                    for c in range(4):
                        n = nb * 4 + c
                        bk = 4 + (c % 2)
                        for kc in range(KC):
                            mm(PB[bk][:, :], WRs(so)[:, kc * 512 + c * 128: kc * 512 + (c + 1) * 128], Hk(kc),
                               kc == 0, kc == KC - 1, [tWR[so], tH[kc]], [tPB[bk]])
                        stt(Rk(n), PB[bk][:, :], pv(P_G2)[:, n:n + 1], Rk(n), ALU.mult, ALU.add,
                            [tPB[bk], tPV, tR[n]], [tR[n]])
                if KDUMP == 3:
                    dump_R(t0)
                    continue
                layernorm(P_L2RA, P_L2RB, P_L2HA, P_L2HB)
                ffn(1, P_HG3)
                layernorm(P_L3RA, P_L3RB, None, None)
                dump_R(t0)

        import os
        stage = int(os.environ.get("KSTAGE", "99"))
        phase0()
        S.barrier()
        if stage >= 1:
            phaseA()
            S.barrier()
        if stage >= 2:
            exchange_halo()
            gla_pass(False)
            exchange_state()
            S.barrier()
        if stage >= 3:
            gla_pass(True)
            S.barrier()
        if stage >= 4:
            attention()
            S.barrier()
        if stage >= 5:
            phaseC()
            S.barrier()
        emit_all(nc, S, es)
    return nc


_NC_CACHE = {}


def kernel(x, c, positions, w_ada, b_ada, ln1_g, ln1_b, w_ffn1_gu, w_ffn1_down, w_in, w_alpha2, b_alpha,
           gla_norm_g, w_branch_a, w_branch_b, w_out, ln2_g, ln2_b, w_ffn2_gu, w_ffn2_down, ln3_g, ln3_b):
    f = lambda a: np.ascontiguousarray(np.asarray(a, dtype=np.float32))
    x = f(x)
    c = f(c)
    positions = np.ascontiguousarray(np.asarray(positions, dtype=np.int32))
    lay = lambda v: np.ascontiguousarray(f(v).reshape(-1, 128).T)
    vecs = np.stack([lay(ln1_g[0]), lay(ln1_b[0]), lay(ln2_g[0]), lay(ln2_b[0]), lay(ln3_g[0]), lay(ln3_b[0]),
                     lay(f(gla_norm_g[0]).reshape(-1))], axis=1)
    wal = np.zeros((32, 1024), np.float32)
    wal[0:16] = f(w_alpha2[0])
    wal[16] = f(b_alpha[0])
    s_i = np.arange(128)[:, None]
    c_i = np.arange(128)[None, :]
    same = (s_i // 64) == (c_i // 64)
    consts = np.zeros((128, 8, 128), np.float32)
    consts[:, 0] = np.eye(128)
    consts[:, 1] = np.roll(np.eye(128), 64, axis=0)
    consts[:, 2] = 1.0 / D
    consts[:, 3] = np.where(same & (s_i > c_i), -1.0 / 16.0, 0.0)
    consts[:, 4] = np.where(same & (s_i <= c_i), -1.0 / 16.0, 0.0)
    consts[:, 5] = np.where(same & (s_i <= c_i), 1.0, 0.0)
    consts[:, 6, 0] = np.where(np.arange(128) < 64, -1.0 / 16.0, 0.0)
    consts[:, 6, 1] = np.where(np.arange(128) >= 64, -1.0 / 16.0, 0.0)
    half = 64
    freq = (10000.0 ** (-np.arange(half, dtype=np.float32) / half)).astype(np.float32)
    consts[:, 6, 2] = np.concatenate([freq, freq])
    consts[:, 6, 3] = np.where(np.arange(128) < 64, -1.0, 1.0)
    consts[:, 6, 5] = LN_EPS
    consts[:, 6, 6] = 0.0
    consts[:, 6, 7] = np.float32(math.pi / 2)
    j_i = np.arange(128)[:, None]
    q_i = np.arange(128)[None, :]
    mprev = (j_i >= q_i).astype(np.float32)
    mcur = (j_i <= q_i).astype(np.float32)
    shared = {
        "w_ada": f(w_ada[0]), "w_ffn1_gu": f(w_ffn1_gu[0]), "w_ffn2_gu": f(w_ffn2_gu[0]),
        "w_ffn1_down": f(w_ffn1_down[0]), "w_ffn2_down": f(w_ffn2_down[0]), "w_in": f(w_in[0]),
        "w_alpha_aug": wal, "w_branch_a": f(w_branch_a[0]), "w_branch_b": f(w_branch_b[0]), "w_out": f(w_out[0]),
        "vecs": np.ascontiguousarray(vecs), "b_ada_l": lay(b_ada[0]),
    }
    in_maps = []
    for core in range(8):
        b, hf = core // 2, core % 2
        cst = consts.copy()
        cst[:, 6, 4] = float(hf)
        masks = np.stack([np.concatenate([mprev * float(hf), mcur], 1), np.concatenate([mprev, mcur], 1)], 1)
        m = dict(shared)
        m["x"] = np.ascontiguousarray(x[b, hf * NTOK:(hf + 1) * NTOK])
        m["c_l"] = lay(c[b])
        m["pos"] = np.ascontiguousarray(positions[b, hf * NTOK:(hf + 1) * NTOK][None, :])
        m["consts"] = cst
        m["masks"] = np.ascontiguousarray(masks.astype(np.float32))
        in_maps.append(m)
    if "nc" not in _NC_CACHE:
        _NC_CACHE["nc"] = build_nc()
    import os
    ncores = int(os.environ.get("KCORES", "8"))
    res = run_bass_kernel_spmd(_NC_CACHE["nc"], in_maps[:ncores], core_ids=list(range(ncores)))
    out = np.zeros((BATCH, SEQ, D), np.float32)
    for core in range(ncores):
        b, hf = core // 2, core % 2
        out[b, hf * NTOK:(hf + 1) * NTOK] = res.results[core]["out"]
    return out
```

```python
import math
import numpy as np
from contextlib import ExitStack
import concourse.bass as bass
import concourse.mybir as mybir
from concourse.bass_utils import run_bass_kernel_spmd

F32 = mybir.dt.float32
BF16 = mybir.dt.bfloat16
I32 = mybir.dt.int32
AF = mybir.ActivationFunctionType
ALU = mybir.AluOpType

D = 2048
KC = 16
SEQ = 8192
BATCH = 4
NTOK = 4096
T = 512
NT = NTOK // T
DFF = 5632
FC = DFF // 128
INW = 19472
LN_EPS = 1e-5
ALPHA = 2.0 ** 0.25
CQ, CK, CV, CR, CDQ, CDK, CDV, CGA, CGB = 0, 8, 16, 32, 48, 72, 96, 120, 136
NCH = 152
WIN_BLOCK_COLS = [c for c in range(0, 6144, 512)] + [c for c in range(6160, INW, 512)]
assert len(WIN_BLOCK_COLS) == 38
DIL = (1, 4, 16)

ENGS = ("pe", "act", "dve", "pool", "sp")
DMA_K = 8
NWR = 4


class Op:
    __slots__ = ("eng", "fn", "deps", "signal", "count", "sem", "is_dma", "pre_wait", "idx")

    def __init__(self, eng, fn, is_dma=False):
        self.eng = eng
        self.fn = fn
        self.deps = []
        self.signal = False
        self.count = 0
        self.sem = None
        self.is_dma = is_dma
        self.pre_wait = None
        self.idx = 0


class Tok:
    __slots__ = ("w", "r")

    def __init__(self):
        self.w = None
        self.r = []


class Sched:
    def __init__(self, dma_queues=("sp", "pool")):
        self.prog = {e: [] for e in ENGS}
        self.dma_queues = dma_queues
        self.last = {e: None for e in ENGS}

    def _add(self, op, reads, writes):
        best = {}
        dmas = {}

        def add(d):
            if d is None:
                return
            if d.is_dma:
                dmas[id(d)] = d
            else:
                b = best.get(d.eng)
                if b is None or d.idx > b.idx:
                    best[d.eng] = d

        for t in reads:
            add(t.w)
        for t in writes:
            add(t.w)
            for r in t.r:
                add(r)
        op.deps = list(best.values()) + list(dmas.values())
        op.idx = len(self.prog[op.eng])
        for t in reads:
            if (not op.is_dma) and t.r and (not t.r[-1].is_dma) and t.r[-1].eng == op.eng:
                t.r[-1] = op
            else:
                t.r.append(op)
        for t in writes:
            t.w = op
            t.r = []
        self.prog[op.eng].append(op)
        if op.fn is not None:
            self.last[op.eng] = op
        return op

    def op(self, eng, fn, reads=(), writes=()):
        return self._add(Op(eng, fn), reads, writes)

    def dma(self, q, fn, reads=(), writes=()):
        return self._add(Op(q, fn, is_dma=True), reads, writes)

    def barrier(self):
        lasts = [self.last[e] for e in ENGS if self.last[e] is not None and not self.last[e].is_dma]
        for q in self.dma_queues:
            lasts.extend([o for o in self.prog[q] if o.is_dma][-DMA_K:])
        for e in ENGS:
            comp = [o for o in self.prog[e][-4000:] if (not o.is_dma) and o.fn is not None]
            if comp and comp[-1] not in lasts:
                lasts.append(comp[-1])
        for e in ENGS:
            op = Op(e, None)
            op.deps = list(lasts)
            op.idx = len(self.prog[e])
            self.prog[e].append(op)

    def emit(self, sems):
        for e in ENGS:
            for op in self.prog[e]:
                for d in op.deps:
                    if d.is_dma:
                        continue
                    if d.eng == "pe" and op.eng == "pe" and not op.is_dma and op.fn is not None:
                        continue
                    d.signal = True
        for e in ENGS:
            c = 0
            dn = 0
            for op in self.prog[e]:
                if op.is_dma:
                    k = dn % DMA_K
                    op.sem = sems[("dma", e, k)]
                    need = 16 * (dn // DMA_K)
                    op.pre_wait = (op.sem, need) if need > 0 else None
                    op.count = need + 16
                    dn += 1
                elif op.signal:
                    c += 1
                    op.count = c
                    op.sem = sems[e]
        out = {}
        for e in ENGS:
            waited = {}
            lst = []
            for op in self.prog[e]:
                for d in op.deps:
                    if (not d.is_dma) and d.eng == "pe" and e == "pe" and (not op.is_dma) and op.fn is not None:
                        continue
                    key = id(d.sem)
                    if waited.get(key, 0) >= d.count:
                        continue
                    waited[key] = d.count
                    lst.append(("w", d.sem, d.count))
                if op.is_dma and op.pre_wait is not None:
                    key = id(op.pre_wait[0])
                    if waited.get(key, 0) < op.pre_wait[1]:
                        waited[key] = op.pre_wait[1]
                        lst.append(("w", op.pre_wait[0], op.pre_wait[1]))
                if op.fn is not None:
                    lst.append(("o", op))
            out[e] = lst
        return out


def emit_all(nc, sched, es):
    sems = {}
    for e in ENGS:
        sems[e] = es.enter_context(nc.semaphore(f"s_{e}"))
    for q in sched.dma_queues:
        for k in range(DMA_K):
            sems[("dma", q, k)] = es.enter_context(nc.semaphore(f"d_{q}_{k}"))
    lists = sched.emit(sems)

    def run(eng, lst):
        for it in lst:
            if it[0] == "w":
                eng.wait_ge(it[1], it[2])
            else:
                op = it[1]
                ins = op.fn(eng)
                if op.is_dma:
                    ins.then_inc(op.sem, 16)
                elif op.signal:
                    ins.then_inc(op.sem, 1)

    with nc.Block() as block:
        @block.tensor
        def _(e):
            run(e, lists["pe"])

        @block.scalar
        def _(e):
            run(e, lists["act"])

        @block.vector
        def _(e):
            run(e, lists["dve"])

        @block.gpsimd
        def _(e):
            run(e, lists["pool"])

        @block.sync
        def _(e):
            run(e, lists["sp"])


def build_nc():
    import os
    KSUB = int(os.environ.get("KSUB", "99"))
    KDBG = int(os.environ.get("KDBG", "99"))
    KDUMP = int(os.environ.get("KDUMP", "0"))
    nc = bass.Bass("TRN2", target_bir_lowering=False)
    dt_in = lambda name, shape, dt=F32: nc.dram_tensor(name, shape, dt, kind="ExternalInput").ap()
    x_d = dt_in("x", [NTOK, D])
    c_d = dt_in("c_l", [128, KC])
    pos_d = dt_in("pos", [1, NTOK], I32)
    wada_d = dt_in("w_ada", [D, 9 * D])
    bada_d = dt_in("b_ada_l", [128, 144])
    wgu_d = [dt_in("w_ffn1_gu", [D, 2 * DFF]), dt_in("w_ffn2_gu", [D, 2 * DFF])]
    wdn_d = [dt_in("w_ffn1_down", [DFF, D]), dt_in("w_ffn2_down", [DFF, D])]
    win_d = dt_in("w_in", [D, INW])
    wal_d = dt_in("w_alpha_aug", [32, 1024])
    wba_d = dt_in("w_branch_a", [D, D])
    wbb_d = dt_in("w_branch_b", [1024, D])
    wout_d = dt_in("w_out", [D, D])
    vecs_d = dt_in("vecs", [128, 7, KC])
    cst_d = dt_in("consts", [128, 8, 128])
    msk_d = dt_in("masks", [128, 2, 256])
    out_d = nc.dram_tensor("out", [NTOK, D], F32, kind="ExternalOutput").ap()

    dram = lambda name, shape, dt: nc.dram_tensor(name, shape, dt).ap()
    wgu_s = [dram("wgu1_s", [22, 128, KC, 512], BF16), dram("wgu2_s", [22, 128, KC, 512], BF16)]
    wdn_s = [dram("wdn1_s", [16, 128, FC, 128], BF16), dram("wdn2_s", [16, 128, FC, 128], BF16)]
    win_s = dram("win_s", [38, 128, KC, 512], BF16)
    wba_s = dram("wba_s", [4, 128, KC, 512], BF16)
    wbb_s = dram("wbb_s", [4, 128, 8, 512], BF16)
    wout_s = dram("wout_s", [4, 128, KC, 512], BF16)
    projT = dram("projT", [NCH * 128, NTOK], BF16)
    glrT = dram("glrT", [16, NTOK], F32)
    x1a = dram("x1a", [D, NTOK], F32)
    oaT = dram("oaT", [D, NTOK], BF16)
    obT = dram("obT", [1024, NTOK], BF16)
    cosT = dram("cosT", [128, NTOK], F32)
    sinT = dram("sinT", [128, NTOK], F32)
    XROWS = 8192 + 2048 + 8192 + 32768
    XIN = nc.dram_tensor("xin", [XROWS, 128], BF16)
    XOUT = nc.dram_tensor("xout", [8 * XROWS, 128], BF16)
    XLOC = nc.dram_tensor("xloc", [XROWS, 128], BF16)
    XOFF = {"st": (0, 8192, 8), 0: (8192, 2048, 1), 1: (10240, 8192, 4), 2: (18432, 32768, 16)}

    def xview(t, key, base=0):
        o, n, b = XOFF[key]
        ap = t.ap()[base + o: base + o + n, :]
        return ap.rearrange("(a b) c -> a (b c)", b=b) if b > 1 else ap

    slot_cache = {}

    def xview_dyn(e, key, row0=0, nrows=None):
        o, n, b = XOFF[key]
        if nrows is None:
            nrows = n // b
        if id(e) not in slot_cache:
            slot_cache[id(e)] = e.snap(((e.partition_id() // 2) * 2) * XROWS)
        base = slot_cache[id(e)]
        ap = XOUT.ap()[bass.ds(base, XROWS), :][o + row0 * b: o + row0 * b + nrows * b, :]
        return ap.rearrange("(a b) c -> a (b c)", b=b) if b > 1 else ap

    with ExitStack() as es:
        sbt = lambda name, shape, dt: es.enter_context(nc.sbuf_tensor(name, shape, dt))
        R = sbt("R", [128, 8192], F32)
        H = sbt("H", [128, 8192], BF16)
        AT = sbt("AT", [128, 22528], BF16)
        WR = sbt("WR", [128, NWR * 8192], BF16)
        SQ = sbt("SQ", [128, 1024], F32)
        ST3 = sbt("ST3", [128, 3 * 512], F32)
        TMPF = sbt("TMPF", [128, 1024], F32)
        STG = sbt("STG", [128, 3 * 2048], BF16)
        CS = sbt("CS", [128, 2 * 1024], F32)
        TSB = sbt("TSB", [128, 1024], F32)
        CST = sbt("CST", [128, 8, 128], F32)
        IDB = sbt("IDB", [128, 128], BF16)
        ONB = sbt("ONB", [128, 128], BF16)
        MSK = sbt("MSK", [128, 2, 256], F32)
        VEC = sbt("VEC", [128, 7, KC], F32)
        PV = sbt("PV", [128, 16, KC], F32)
        MODS = sbt("MODS", [128, 144], F32)
        BADA = sbt("BADA", [128, 144], F32)
        CACT = sbt("CACT", [128, KC], F32)
        GLW = sbt("GLW", [128, KC, 16], F32)
        WAL = STG[0:32, 0:2048].bitcast(F32)
        GLWB = sbt("GLWB", [128, KC, 16], BF16)
        GLST = sbt("GLST", [32, 512], F32)
        SM = sbt("SM", [128, 256], F32)
        PB = [es.enter_context(nc.psum_tensor(f"pb{i}", [128, 512], F32)) for i in range(8)]

        ATf = AT[:, 0:22528].bitcast(F32)
        IDENT, PERM, ONESM, USUF, UTRI, MASKA = (CST[:, i, :] for i in range(6))
        CIND = CST[:, 6, 0:2]
        FREQ = CST[:, 6, 2:3]
        SGN = CST[:, 6, 3:4]
        FLAG = CST[:, 6, 4:5]
        EPS = CST[:, 6, 5:6]
        ZERO = CST[:, 6, 6:7]
        HALFPI = CST[:, 6, 7:8]

        S = Sched()
        tR = [Tok() for _ in range(KC)]
        tH = [Tok() for _ in range(KC)]
        tAT = [Tok() for _ in range(FC)]
        tWR = [Tok() for _ in range(NWR)]
        tPB = [Tok() for _ in range(8)]
        tSTG = [Tok() for _ in range(3)]
        tSQ = [Tok(), Tok()]
        tST = [Tok(), Tok(), Tok()]
        tTMP = [Tok(), Tok()]
        tCS = [Tok(), Tok()]
        tTSB = [Tok(), Tok()]
        tC = Tok()
        tPV = Tok()
        tSM = Tok()
        tGLRS = Tok()
        tGLST = Tok()

        cnt = {"wr": 0, "stg": 0}

        def ld(out, in_, writes, reads=(), q="sp"):
            return S.dma(q, lambda e: e.dma_start(out=out, in_=in_), reads=reads, writes=writes)

        def st(out, in_, reads, writes=(), q="pool"):
            return S.dma(q, lambda e: e.dma_start(out=out, in_=in_), reads=reads, writes=writes)

        def mm(out, lhsT, rhs, start, stop, reads, writes):
            return S.op("pe", lambda e: e.matmul(out, lhsT=lhsT, rhs=rhs, start=start, stop=stop), reads, writes)

        def tr(out, in_, ident, reads, writes):
            return S.op("pe", lambda e: e.transpose(out, in_, ident), reads, writes)

        def act(out, in_, func, reads, writes, bias=None, scale=1.0):
            if bias is None:
                return S.op("act", lambda e: e.activation(out=out, in_=in_, func=func, scale=scale), reads, writes)
            return S.op("act", lambda e: e.activation(out=out, in_=in_, func=func, bias=bias, scale=scale), reads,
                        writes)

        def tt(out, in0, in1, op, reads, writes, eng="dve"):
            return S.op(eng, lambda e: e.tensor_tensor(out=out, in0=in0, in1=in1, op=op), reads, writes)

        def ts(out, in0, s1, s2, op0, op1, reads, writes, eng="dve"):
            return S.op(eng, lambda e: e.tensor_scalar(out=out, in0=in0, scalar1=s1, scalar2=s2, op0=op0, op1=op1),
                        reads, writes)

        def tss(out, in0, s1, op0, reads, writes, eng="dve"):
            return S.op(eng, lambda e: e.tensor_single_scalar(out=out, in_=in0, scalar=s1, op=op0), reads, writes)

        def stt(out, in0, scalar, in1, op0, op1, reads, writes, eng="dve"):
            return S.op(eng, lambda e: e.scalar_tensor_tensor(out=out, in0=in0, scalar=scalar, in1=in1, op0=op0,
                                                              op1=op1), reads, writes)

        def cp(out, in_, reads, writes, eng="dve"):
            if eng == "act":
                return S.op("act", lambda e: e.copy(out=out, in_=in_), reads, writes)
            return S.op(eng, lambda e: e.tensor_copy(out=out, in_=in_), reads, writes)

        def Rk(kc, a=0, b=T):
            return R[:, kc * T + a: kc * T + b]

        def Hk(kc, a=0, b=T):
            return H[:, kc * T + a: kc * T + b]

        def ATk(j, a=0, b=T):
            return AT[:, j * T + a: j * T + b]

        def WRs(s):
            return WR[:, s * 8192:(s + 1) * 8192]

        def pv(i):
            return PV[:, i, :]

        (P_A1, P_B1, P_HG1, P_L1RA, P_L1RB, P_L1HA, P_L1HB, P_G2, P_L2RA, P_L2RB, P_L2HA, P_L2HB, P_HG3, P_L3RA,
         P_L3RB, P_GG) = range(16)

        def phase0():
            ld(CST[:], cst_d, [tC])
            ld(MSK[:], msk_d, [tC])
            ld(VEC[:], vecs_d, [tC])
            ld(BADA[:], bada_d, [tC])
            ld(CACT[:], c_d, [tC])
            ld(GLW[:], win_d[:, 6144:6160].rearrange("(kc p) n -> p kc n", p=128), [tC])
            cp(GLWB[:], GLW[:], [tC], [tC])
            cp(IDB[:], IDENT, [tC], [tC])
            S.op("dve", lambda e: e.memset(ONB[:], 1.0), [], [tC])
            act(CACT[:], CACT[:], AF.Silu, [tC], [tC])
            stg = [(R[:, 0:8192], tR), (ATf[:, 0:8192], tAT)]
            for jb in range(36):
                buf, toks = stg[jb % 2]
                ld(buf.rearrange("p (kc n) -> p kc n", kc=KC),
                   wada_d[:, jb * 512:(jb + 1) * 512].rearrange("(kc p) n -> p kc n", p=128), toks)
                for mc in range(4):
                    col = jb * 4 + mc
                    for kc in range(KC):
                        mm(PB[0][:, col:col + 1], buf[:, kc * 512 + mc * 128: kc * 512 + (mc + 1) * 128],
                           CACT[:, kc:kc + 1], kc == 0, kc == KC - 1, list(toks) + [tC], [tPB[0]])
            tt(MODS[:], PB[0][:, 0:144], BADA[:], ALU.add, [tPB[0], tC], [tPV])
            md = lambda m: MODS[:, m * 16:(m + 1) * 16]
            g = lambda i: VEC[:, i, :]
            rw = ([tPV, tC], [tPV])
            ts(pv(P_A1), md(1), 1.0, 1.0 / ALPHA, ALU.add, ALU.mult, *rw)
            cp(pv(P_B1), md(0), *rw)
            tss(pv(P_HG1), md(2), 0.5, ALU.mult, *rw)
            tss(pv(P_L1RA), g(0), ALPHA, ALU.mult, *rw)
            tss(pv(P_L1RB), g(1), ALPHA, ALU.mult, *rw)
            tss(pv(P_GG), md(4), 1.0, ALU.add, *rw)
            tt(pv(P_L1HA), g(0), pv(P_GG), ALU.mult, *rw)
            tt(pv(P_L1HB), g(1), pv(P_GG), ALU.mult, *rw)
            tt(pv(P_L1HB), pv(P_L1HB), md(3), ALU.add, *rw)
            cp(pv(P_G2), md(5), *rw)
            tss(pv(P_L2RA), g(2), ALPHA, ALU.mult, *rw)
            tss(pv(P_L2RB), g(3), ALPHA, ALU.mult, *rw)
            tss(pv(P_GG), md(7), 1.0, ALU.add, *rw)
            tt(pv(P_L2HA), g(2), pv(P_GG), ALU.mult, *rw)
            tt(pv(P_L2HB), g(3), pv(P_GG), ALU.mult, *rw)
            tt(pv(P_L2HB), pv(P_L2HB), md(6), ALU.add, *rw)
            tss(pv(P_HG3), md(8), 0.5, ALU.mult, *rw)
            cp(pv(P_L3RA), g(4), *rw)
            cp(pv(P_L3RB), g(5), *rw)
            cp(pv(P_GG), g(6), *rw)
            S.barrier()
            C1 = 6.28125
            C2 = 2.0 * math.pi - C1
            PIC = 3.1415925
            posI = H[:, 0:8192].bitcast(I32)
            tpos = Tok()
            ld(posI[:, 0:NTOK], pos_d.partition_broadcast(128), [tpos], q="pool")
            tl = [Tok() for _ in range(6)]
            for pc in range(4):
                sl = slice(pc * 1024, (pc + 1) * 1024)
                posf = R[:, 0:1024]
                cp(posf, posI[:, sl], [tpos], [tl[0]])
                for which in range(2):
                    t0 = R[:, 1024:2048]
                    tmp = R[:, 2048:3072]
                    nI = R[:, 3072:4096].bitcast(I32)
                    nF = R[:, 4096:5120]
                    r = R[:, 5120:6144]
                    ts(t0, posf, FREQ, HALFPI if which == 0 else ZERO, ALU.mult, ALU.add, [tl[0], tC], [tl[1]])
                    tss(tmp, t0, 1.0 / (2.0 * math.pi), ALU.mult, [tl[1]], [tl[2]])
                    cp(nI, tmp, [tl[2]], [tl[3]])
                    cp(nF, nI, [tl[3]], [tl[4]])
                    stt(r, nF, -C1, t0, ALU.mult, ALU.add, [tl[4], tl[1]], [tl[5]])
                    stt(r, nF, -C2, r, ALU.mult, ALU.add, [tl[4], tl[5]], [tl[5]])
                    ts(r, r, -PIC, PIC, ALU.max, ALU.min, [tl[5]], [tl[5]])
                    if which == 0:
                        act(r, r, AF.Sin, [tl[5]], [tl[5]])
                    else:
                        act(r, r, AF.Sin, [tl[5], tC], [tl[5]], scale=SGN)
                    st((cosT if which == 0 else sinT)[:, sl], r, [tl[5]])
            S.barrier()
            sin_ = [(R[:, 0:8192], Tok()), (ATf[:, 0:8192], Tok())]
            sout = [(H[:, 0:8192], Tok()), (WRs(0), Tok())]
            n = [0]

            def conv(src_ap, dst_ap, kcn, nb):
                i = n[0] % 2
                n[0] += 1
                (bi, ti), (bo, to) = sin_[i], sout[i]
                sz = kcn * nb
                ld(bi[:, 0:sz].rearrange("p (kc n) -> p kc n", kc=kcn),
                   src_ap.rearrange("(kc p) n -> p kc n", p=128), [ti])
                if i == 0:
                    cp(bo[:, 0:sz], bi[:, 0:sz], [ti], [to], eng="dve")
                else:
                    cp(bo[:, 0:sz], bi[:, 0:sz], [ti], [to], eng="act")
                st(dst_ap, bo[:, 0:sz].rearrange("p (kc n) -> p kc n", kc=kcn), [to])

            for l in range(2):
                for b in range(22):
                    conv(wgu_d[l][:, b * 512:(b + 1) * 512], wgu_s[l][b], KC, 512)
                for b in range(16):
                    conv(wdn_d[l][:, b * 128:(b + 1) * 128], wdn_s[l][b], FC, 128)
            for b in range(38):
                c0 = WIN_BLOCK_COLS[b]
                conv(win_d[:, c0:c0 + 512], win_s[b], KC, 512)
            for b in range(4):
                conv(wba_d[:, b * 512:(b + 1) * 512], wba_s[b], KC, 512)
                conv(wbb_d[:, b * 512:(b + 1) * 512], wbb_s[b], 8, 512)
                conv(wout_d[:, b * 512:(b + 1) * 512], wout_s[b], KC, 512)

        def wload(src_block_ap, kcn, nb):
            s = cnt["wr"] % NWR
            cnt["wr"] += 1
            ld(WRs(s)[:, 0:kcn * nb].rearrange("p (kc n) -> p kc n", kc=kcn), src_block_ap, [tWR[s]])
            return s

        def layernorm(ra, rb, ha, hb):
            mean_ps, ex2_ps = PB[6], PB[7]
            for kc in range(KC):
                mm(mean_ps[:, :], ONESM, Rk(kc), kc == 0, kc == KC - 1, [tR[kc], tC], [tPB[6]])
            for kc in range(KC):
                b = kc % 2
                act(SQ[:, b * 512:(b + 1) * 512], Rk(kc), AF.Square, [tR[kc]], [tSQ[b]])
                mm(ex2_ps[:, :], ONESM, SQ[:, b * 512:(b + 1) * 512], kc == 0, kc == KC - 1, [tSQ[b], tC], [tPB[7]])
            mean, sd, rstd = ST3[:, 0:512], ST3[:, 512:1024], ST3[:, 1024:1536]
            cp(mean, mean_ps[:, :], [tPB[6]], [tST[0]])
            tt(sd, mean, mean, ALU.mult, [tST[0]], [tST[1]])
            tt(sd, ex2_ps[:, :], sd, ALU.subtract, [tPB[7], tST[1]], [tST[1]])
            act(sd, sd, AF.Sqrt, [tST[1], tC], [tST[1]], bias=EPS)
            S.op("dve", lambda e: e.reciprocal(out=rstd, in_=sd), [tST[1]], [tST[2]])
            for kc in range(KC):
                b = kc % 2
                tmp = TMPF[:, b * 512:(b + 1) * 512]
                tt(tmp, Rk(kc), mean, ALU.subtract, [tR[kc], tST[0]], [tTMP[b]])
                tt(tmp, tmp, rstd, ALU.mult, [tTMP[b], tST[2]], [tTMP[b]])
                act(Rk(kc), tmp, AF.Identity, [tTMP[b], tPV], [tR[kc]], bias=pv(rb)[:, kc:kc + 1],
                    scale=pv(ra)[:, kc:kc + 1])
                if ha is not None:
                    ts(Hk(kc), tmp, pv(ha)[:, kc:kc + 1], pv(hb)[:, kc:kc + 1], ALU.mult, ALU.add,
                       [tTMP[b], tPV], [tH[kc]])

        def ffn(l, hg):
            for j4 in range(11):
                sg = wload(wgu_s[l][j4], KC, 512)
                su = wload(wgu_s[l][11 + j4], KC, 512)
                for c in range(4):
                    j = j4 * 4 + c
                    bg, bu = (0, 1) if c % 2 == 0 else (2, 3)
                    for kc in range(KC):
                        mm(PB[bg][:, :], WRs(sg)[:, kc * 512 + c * 128: kc * 512 + (c + 1) * 128], Hk(kc), kc == 0,
                           kc == KC - 1, [tWR[sg], tH[kc]], [tPB[bg]])
                    for kc in range(KC):
                        mm(PB[bu][:, :], WRs(su)[:, kc * 512 + c * 128: kc * 512 + (c + 1) * 128], Hk(kc), kc == 0,
                           kc == KC - 1, [tWR[su], tH[kc]], [tPB[bu]])
                    b = j % 2
                    tmp = TMPF[:, b * 512:(b + 1) * 512]
                    act(tmp, PB[bg][:, :], AF.Silu, [tPB[bg]], [tTMP[b]])
                    tt(ATk(j), tmp, PB[bu][:, :], ALU.mult, [tTMP[b], tPB[bu]], [tAT[j]])
            for m in range(KC):
                s = wload(wdn_s[l][m], FC, 128)
                bk = 4 + (m % 2)
                for j in range(FC):
                    mm(PB[bk][:, :], WRs(s)[:, j * 128:(j + 1) * 128], ATk(j), j == 0, j == FC - 1,
                       [tWR[s], tAT[j]], [tPB[bk]])
                stt(Rk(m), PB[bk][:, :], pv(hg)[:, m:m + 1], Rk(m), ALU.mult, ALU.add, [tPB[bk], tPV, tR[m]],
                    [tR[m]])

        def phaseA():
            for it in range(NT):
                t0 = it * T
                cb = it % 2
                csb = CS[:, cb * 1024:(cb + 1) * 1024]
                ld(csb[:, 0:512], cosT[:, t0:t0 + T], [tCS[cb]])
                ld(csb[:, 512:1024], sinT[:, t0:t0 + T], [tCS[cb]])
                xs = [ATf[:, b * 2048:(b + 1) * 2048] for b in range(4)]
                for b in range(4):
                    ld(xs[b], x_d[t0 + b * 128: t0 + (b + 1) * 128, :], tAT[b * 8:(b + 1) * 8])
                for kc in range(KC):
                    bk = kc % 4
                    if KDBG < 2:
                        continue
                    for b in range(4):
                        tr(PB[bk][:, b * 128:(b + 1) * 128], xs[b][:, kc * 128:(kc + 1) * 128], IDENT,
                           tAT[b * 8:(b + 1) * 8] + [tC], [tPB[bk]])
                    act(Rk(kc), PB[bk][:, :], AF.Identity, [tPB[bk]], [tR[kc]], scale=ALPHA)
                    ts(Hk(kc), Rk(kc), pv(P_A1)[:, kc:kc + 1], pv(P_B1)[:, kc:kc + 1], ALU.mult, ALU.add,
                       [tR[kc], tPV], [tH[kc]])
                if KSUB < 2:
                    continue
                ffn(0, P_HG1)
                if KSUB < 3:
                    continue
                if KDUMP == 1:
                    dump_R(t0)
                    continue
                layernorm(P_L1RA, P_L1RB, P_L1HA, P_L1HB)
                if KDUMP == 2:
                    dump_R(t0)
                    continue
                if KSUB < 4:
                    continue
                st(x1a[:, t0:t0 + T].rearrange("(kc p) t -> p kc t", p=128),
                   R[:, 0:8192].rearrange("p (kc t) -> p kc t", kc=KC), list(tR))
                if KSUB < 5:
                    continue
                for kc in range(KC):
                    mm(PB[5][0:16, :], GLWB[:, kc, :], Hk(kc), kc == 0, kc == KC - 1, [tC, tH[kc]], [tPB[5]])
                cp(GLST[0:16, :], PB[5][0:16, :], [tPB[5]], [tGLST])
                st(glrT[:, t0:t0 + T], GLST[0:16, :], [tGLST])
                ch = 0
                for b in range(38):
                    s = wload(win_s[b], KC, 512)
                    sg = cnt["stg"] % 3
                    cnt["stg"] += 1
                    stg = STG[:, sg * 2048:(sg + 1) * 2048]
                    for c in range(4):
                        bk = c % 4
                        for kc in range(KC):
                            mm(PB[bk][:, :], WRs(s)[:, kc * 512 + c * 128: kc * 512 + (c + 1) * 128], Hk(kc),
                               kc == 0, kc == KC - 1, [tWR[s], tH[kc]], [tPB[bk]])
                        o = stg[:, c * 512:(c + 1) * 512]
                        if CR <= ch < CDQ:
                            act(o, PB[bk][:, :], AF.Silu, [tPB[bk]], [tSTG[sg]])
                        elif ch >= CGA:
                            act(o, PB[bk][:, :], AF.Sigmoid, [tPB[bk]], [tSTG[sg]])
                        elif CDQ <= ch < CDV:
                            tb = ch % 2
                            tsb = TSB[:, tb * 512:(tb + 1) * 512]
                            cp(tsb, PB[bk][:, :], [tPB[bk]], [tTSB[tb]], eng="act")
                            mm(PB[4 + tb][:, :], PERM, tsb, True, True, [tC, tTSB[tb]], [tPB[4 + tb]])
                            tt(tsb, tsb, csb[:, 0:512], ALU.mult, [tTSB[tb], tCS[cb]], [tTSB[tb]])
                            tmp = TMPF[:, tb * 512:(tb + 1) * 512]
                            tt(tmp, PB[4 + tb][:, :], csb[:, 512:1024], ALU.mult, [tPB[4 + tb], tCS[cb]], [tTMP[tb]])
                            tt(o, tsb, tmp, ALU.add, [tTSB[tb], tTMP[tb]], [tSTG[sg]])
                        else:
                            if ch % 2 == 0:
                                cp(o, PB[bk][:, :], [tPB[bk]], [tSTG[sg]], eng="act")
                            else:
                                cp(o, PB[bk][:, :], [tPB[bk]], [tSTG[sg]], eng="dve")
                        ch += 1
                    st(projT[(ch - 4) * 128: ch * 128, t0:t0 + T].rearrange("(c p) t -> p c t", p=128),
                       stg.rearrange("p (c t) -> p c t", c=4), [tSTG[sg]])


        thx = [Tok() for _ in range(3)]

        def exchange_halo():
            for g in range(3):
                w = 128 * DIL[g]
                hin = xview(XIN, g)
                st(hin[0:1024, :], projT[(CDK + g * 8) * 128:(CDK + g * 8 + 8) * 128, NTOK - w:NTOK], [], [tst],
                   q="sp")
                st(hin[1024:2048, :], projT[(CDV + g * 8) * 128:(CDV + g * 8 + 8) * 128, NTOK - w:NTOK], [],
                   [tst], q="sp")

        def gla_pass(use_recv, state_only=False):
            Hf = H[:, 0:8192].bitcast(F32)
            GLR_ = [R[0:32, 0:128], R[0:32, 128:256]]
            GLRK_ = [R[0:17, 0:128], R[0:17, 128:256]]
            LG_ = [R[:, 1024:2048], Hf[:, 0:1024]]
            EX_ = [R[:, 2048:3072], Hf[:, 1024:2048]]
            EX2_ = [R[:, 3072:4096], Hf[:, 2048:3072]]
            EX3_ = [SQ[:, 0:1024], TSB[:, 0:1024]]
            S32 = R[:, 4096:8192]
            QD_ = [AT[:, 0:1024], AT[:, 12288:13312]]
            KI_ = [AT[:, 1024:2048], AT[:, 13312:14336]]
            KEND_ = [AT[:, 2048:3072], AT[:, 14336:15360]]
            VTOK_ = [AT[:, 3072:5120], AT[:, 15360:17408]]
            AM_ = [AT[:, 5120:5632], AT[:, 17408:17920]]
            SQB = AT[:, 5632:6144]
            OAB_ = [AT[:, 6144:8192], AT[:, 17920:19968]]
            SBF = AT[:, 8192:12288]
            GLB = [WR[:, 0:12288], WR[:, 12288:24576]]
            DEC_ = [SM[:, 0:16], SM[:, 16:32]]
            RST = SM[:, 128:256]
            tkp = [{k: Tok() for k in ["glr", "lg", "ex", "ex2", "ex3", "qd", "ki", "kend", "vtok", "am", "oab",
                                       "dec"]} for _ in range(2)]
            tks = {k: Tok() for k in ["sqb", "rst", "glb0", "glb1", "tmp"]}
            tS32 = [Tok() for _ in range(8)]
            tSBF = [Tok() for _ in range(8)]
            tO = tPB[0:4]
            S.op("dve", lambda e: e.memset(R[0:32, 0:256], 1.0), [], [tkp[0]["glr"], tkp[1]["glr"]])
            ld(WAL, wal_d, [tGLRS])
            if use_recv:
                ld(S32.bitcast(BF16).rearrange("p (c n) -> p c n", c=8),
                   xview(XLOC, "st").rearrange("(c p) n -> p c n", p=128), tS32, reads=[tst])
                for c8 in range(8):
                    tss(S32[:, c8 * 512:(c8 + 1) * 512], S32[:, c8 * 512:(c8 + 1) * 512], FLAG, ALU.mult,
                        [tS32[c8], tC], [tS32[c8]])
            else:
                for c8 in range(8):
                    S.op("dve", lambda e, c8=c8: e.memset(S32[:, c8 * 512:(c8 + 1) * 512], 0.0), [], [tS32[c8]])
            for c8 in range(8):
                cp(SBF[:, c8 * 512:(c8 + 1) * 512], S32[:, c8 * 512:(c8 + 1) * 512], [tS32[c8]], [tSBF[c8]], eng="act")

            def blk_ctx(bi):
                    tb = bi * 128
                    p_ = bi % 2
                    tk = dict(tks)
                    tk.update(tkp[p_])
                    GLR, GLRK, LG, EX, EX2, EX3 = GLR_[p_], GLRK_[p_], LG_[p_], EX_[p_], EX2_[p_], EX3_[p_]
                    QD, KI, KEND, VTOK, AM, OAB, DEC = QD_[p_], KI_[p_], KEND_[p_], VTOK_[p_], AM_[p_], OAB_[p_], DEC_[p_]
                    sb_i = (bi // 2) % 2
                    glb = GLB[sb_i].rearrange("p (c t) -> p c t", c=48)
                    tglb = tk[f"glb{sb_i}"]
                    so_ = (bi % 2) * 128
                    qv = lambda c: glb[:, CQ + c, so_:so_ + 128]
                    kv = lambda c: glb[:, CK + c, so_:so_ + 128]
                    vv = lambda c: glb[:, CV + c, so_:so_ + 128]
                    rv = lambda c: glb[:, CR + c, so_:so_ + 128]
                    return locals()

            def blk_prep(bi):
                L = blk_ctx(bi)
                tb, tk, glb, tglb, so_, qv, kv, vv, rv = (L[k] for k in ('tb', 'tk', 'glb', 'tglb', 'so_', 'qv', 'kv', 'vv', 'rv'))
                GLR, GLRK, LG, EX, EX2, EX3 = (L[k] for k in ('GLR', 'GLRK', 'LG', 'EX', 'EX2', 'EX3'))
                QD, KI, KEND, VTOK, AM, OAB, DEC = (L[k] for k in ('QD', 'KI', 'KEND', 'VTOK', 'AM', 'OAB', 'DEC'))
                if bi % 2 == 0:
                    t2 = (bi // 2) * 256
                    ld(glb, projT[0:48 * 128, t2:t2 + 256].rearrange("(c p) t -> p c t", p=128), [tglb],
                       reads=[tproj])
                ld(GLR[0:16, :], glrT[:, tb:tb + 128], [tk["glr"]], reads=[tproj])
                for n2 in range(2):
                    bk = 5 + n2
                    mm(PB[bk][:, :], GLRK, WAL[0:17, n2 * 512:(n2 + 1) * 512], True, True,
                       [tk["glr"], tC, tGLRS], [tPB[bk]])
                    act(EX[:, n2 * 512:(n2 + 1) * 512], PB[bk][:, :], AF.Exp, [tPB[bk]], [tk["ex"]], scale=-1.0)
                    act(LG[:, n2 * 512:(n2 + 1) * 512], EX[:, n2 * 512:(n2 + 1) * 512], AF.Ln, [tk["ex"]],
                        [tk["lg"]], bias=1.0)
                for n2 in range(2):
                    bk = 5 + n2
                    mm(PB[bk][:, :], USUF, LG[:, n2 * 512:(n2 + 1) * 512], True, True, [tC, tk["lg"]], [tPB[bk]])
                    act(EX[:, n2 * 512:(n2 + 1) * 512], PB[bk][:, :], AF.Exp, [tPB[bk]], [tk["ex"]])
                p7b = PB[7][:].bitcast(BF16)
                for c in range(8):
                    tr(p7b[:, c * 128:(c + 1) * 128], kv(c), IDB[:], [tglb, tC], [tPB[7]])
                tt(KEND, p7b[:, 0:1024], EX, ALU.mult, [tPB[7], tk["ex"]], [tk["kend"]])
                for hf in range(2):
                    pb = PB[5 + hf][:].bitcast(BF16)
                    for c in range(8):
                        tr(pb[:, c * 128:(c + 1) * 128], vv(hf * 8 + c), IDB[:], [tglb, tC], [tPB[5 + hf]])
                    cp(VTOK[:, hf * 1024:(hf + 1) * 1024], pb[:, 0:1024], [tPB[5 + hf]], [tk["vtok"]], eng="act")
                if not state_only:
                    for hf in range(2):
                        bk = 5 + hf
                        for c in range(4):
                            cc_ = hf * 4 + c
                            mm(PB[bk][:, c * 128:(c + 1) * 128], LG[:, cc_ * 128:(cc_ + 1) * 128], UTRI, True, True,
                               [tk["lg"], tC], [tPB[bk]])
                        act(EX2[:, hf * 512:(hf + 1) * 512], PB[bk][:, :], AF.Exp, [tPB[bk]], [tk["ex2"]])
                        act(EX3[:, hf * 512:(hf + 1) * 512], PB[bk][:, :], AF.Exp, [tPB[bk]], [tk["ex3"]], scale=-1.0)
                    for c in range(8):
                        stt(QD[:, c * 128:(c + 1) * 128], qv(c), 1.0 / 16.0, EX2[:, c * 128:(c + 1) * 128], ALU.mult,
                            ALU.mult, [tglb, tk["ex2"]], [tk["qd"]])
                        tt(KI[:, c * 128:(c + 1) * 128], kv(c), EX3[:, c * 128:(c + 1) * 128], ALU.mult,
                           [tglb, tk["ex3"]], [tk["ki"]])
                for c in range(8):
                    mm(PB[7][:, c * 2:c * 2 + 2], LG[:, c * 128:(c + 1) * 128], CIND, True, True, [tk["lg"], tC],
                       [tPB[7]])
                act(DEC, PB[7][:, 0:16], AF.Exp, [tPB[7]], [tk["dec"]])
                if not state_only:
                    for h in range(4):
                        for dc in range(2):
                            c = h * 2 + dc
                            mm(PB[4][:, h * 128:(h + 1) * 128], KI[:, c * 128:(c + 1) * 128],
                               QD[:, c * 128:(c + 1) * 128], dc == 0, dc == 1, [tk["ki"], tk["qd"]], [tPB[4]])
                    for h in range(4):
                        tt(AM[:, h * 128:(h + 1) * 128], PB[4][:, h * 128:(h + 1) * 128], MASKA, ALU.mult,
                           [tPB[4], tC], [tk["am"]])

            def blk_finish(bi):
                L = blk_ctx(bi)
                tb, tk, glb, tglb, so_, qv, kv, vv, rv = (L[k] for k in ('tb', 'tk', 'glb', 'tglb', 'so_', 'qv', 'kv', 'vv', 'rv'))
                GLR, GLRK, LG, EX, EX2, EX3 = (L[k] for k in ('GLR', 'GLRK', 'LG', 'EX', 'EX2', 'EX3'))
                QD, KI, KEND, VTOK, AM, OAB, DEC = (L[k] for k in ('QD', 'KI', 'KEND', 'VTOK', 'AM', 'OAB', 'DEC'))
                if not state_only:
                    for h in range(4):
                        for ec in range(4):
                            mm(PB[h][:, ec * 128:(ec + 1) * 128], VTOK[:, (h * 4 + ec) * 128:(h * 4 + ec + 1) * 128],
                               AM[:, h * 128:(h + 1) * 128], ec == 0, False, [tk["vtok"], tk["am"]], [tO[h]])
                for j in range(2):
                    for h in range(4 if not state_only else 0):
                        for ec in range(4):
                            for dc in range(2):
                                c = h * 2 + dc
                                mm(PB[h][:, ec * 128 + 64 * j: ec * 128 + 64 * j + 64],
                                   SBF[:, c * 512 + ec * 128: c * 512 + (ec + 1) * 128],
                                   QD[:, c * 128 + 64 * j: c * 128 + 64 * j + 64], False,
                                   (j == 1 and dc == 1), [tSBF[c], tk["qd"]], [tO[h]])
                    for h in range(4):
                        for dc in range(2):
                            c = h * 2 + dc
                            bk = 5 + (c % 2)
                            mm(PB[bk][:, :], KEND[64 * j:64 * j + 64, c * 128:(c + 1) * 128],
                               VTOK[64 * j:64 * j + 64, h * 512:(h + 1) * 512], True, True,
                               [tk["kend"], tk["vtok"]], [tPB[bk]])
                            stt(S32[:, c * 512:(c + 1) * 512], S32[:, c * 512:(c + 1) * 512],
                                DEC[:, c * 2 + j: c * 2 + j + 1], PB[bk][:, :], ALU.mult, ALU.add,
                                [tS32[c], tk["dec"], tPB[bk]], [tS32[c]])
                            cp(SBF[:, c * 512:(c + 1) * 512], S32[:, c * 512:(c + 1) * 512], [tS32[c]], [tSBF[c]],
                               eng="act")
                if state_only:
                    return
                for h in range(4):
                    act(SQB, PB[h][:, :], AF.Square, [tO[h]], [tk["sqb"]])
                    for ec in range(4):
                        mm(PB[7][:, 0:128], ONB[:], SQB[:, ec * 128:(ec + 1) * 128], ec == 0, ec == 3,
                           [tC, tk["sqb"]], [tPB[7]])
                    act(RST, PB[7][:, 0:128], AF.Sqrt, [tPB[7], tC], [tk["rst"]], bias=EPS, scale=1.0 / 512.0)
                    S.op("dve", lambda e: e.reciprocal(out=RST, in_=RST), [tk["rst"]], [tk["rst"]])
                    for ec in range(4):
                        i = h * 4 + ec
                        tmp = TMPF[:, ec * 128:(ec + 1) * 128]
                        tt(tmp, PB[h][:, ec * 128:(ec + 1) * 128], RST, ALU.mult, [tO[h], tk["rst"]], [tk["tmp"]])
                        stt(OAB[:, i * 128:(i + 1) * 128], tmp, pv(P_GG)[:, i:i + 1], rv(i), ALU.mult, ALU.mult,
                            [tk["tmp"], tPV, tglb], [tk["oab"]])
                st(oaT[:, tb:tb + 128].rearrange("(c p) t -> p c t", p=128),
                   OAB.rearrange("p (c t) -> p c t", c=16), [tk["oab"]], [toa])

            NB_ = NTOK // 128
            blk_prep(0)
            for bi in range(NB_):
                if bi + 1 < NB_:
                    blk_prep(bi + 1)
                blk_finish(bi)
            st(xview(XIN, "st").rearrange("(c p) n -> p c n", p=128),
               S32.bitcast(BF16).rearrange("p (c n) -> p c n", c=8), tS32, [tst])

        tst = Tok()
        tproj = Tok()
        toa = Tok()

        def exchange_state():
            def cc(e):
                return e.collective_compute("AllGather", ALU.bypass, replica_groups=[list(range(8))],
                                            ins=[XIN.ap().opt()], outs=[XOUT.ap().opt()])
            S.op("pool", cc, [tst], [tst])

            def pick(e):
                base = e.snap(((e.partition_id() // 2) * 2) * XROWS)
                return e.dma_start(out=XLOC.ap(), in_=XOUT.ap()[bass.ds(base, XROWS), :])
            S.dma("sp", pick, reads=[tst], writes=[tst])

        def attention():
            OACC = R[:, 0:4096]
            DACC = R[:, 4096:8192]
            bufs = []
            for base, tot in ((WR, 0), (AT, 0)):
                bufs.append((base[:, 0:6144], base[:, 6144:12288], base[:, 12288:16384]))
            PEX = WR[:, 16384:16384 + 512].bitcast(F32)
            PMK = WR[:, 17408:17408 + 256]
            VTK = [WR[:, 18432:18560], WR[:, 18560:18688]]
            OBS = AT[:, 16384:20480]
            tb_ = [Tok(), Tok()]
            tpex, tpmk, tobs, tacc = Tok(), Tok(), Tok(), Tok()
            tvtk = [Tok(), Tok()]
            scale = 128.0 ** -0.5
            n_ld = 0
            for h in range(8):
                for g in range(3):
                    r = DIL[g]
                    w = 128 * r
                    KT, VT, QT = bufs[n_ld % 2]
                    tbuf = tb_[n_ld % 2]
                    n_ld += 1
                    ho = xview(XLOC, g)
                    ld(KT[:, 0:w], ho[h * 128:(h + 1) * 128, :], [tbuf], reads=[tst])
                    ld(VT[:, 0:w], ho[1024 + h * 128:1024 + (h + 1) * 128, :], [tbuf], reads=[tst])
                    ld(KT[:, w:w + NTOK], projT[(CDK + g * 8 + h) * 128:(CDK + g * 8 + h + 1) * 128, :], [tbuf],
                       reads=[tproj])
                    ld(VT[:, w:w + NTOK], projT[(CDV + g * 8 + h) * 128:(CDV + g * 8 + h + 1) * 128, :], [tbuf],
                       reads=[tproj])
                    ld(QT[:, 0:NTOK], projT[(CDQ + g * 8 + h) * 128:(CDQ + g * 8 + h + 1) * 128, :], [tbuf],
                       reads=[tproj])
                    nblk = NTOK // w
                    u = 0
                    for rho in range(r):
                        for n in range(nblk):
                            k0 = w * n + rho
                            q0 = w * n + rho
                            vprev, vcur = VTK[n % 2], VTK[(n + 1) % 2]
                            tvp, tvc = tvtk[n % 2], tvtk[(n + 1) % 2]
                            p7b = PB[7][:].bitcast(BF16)
                            if n == 0:
                                tr(p7b[:, 0:128], VT[:, k0: k0 + 127 * r + 1: r], IDB[:], [tbuf, tC], [tPB[7]])
                                cp(vprev, p7b[:, 0:128], [tPB[7]], [tvp], eng="act")
                            p6b = PB[6][:].bitcast(BF16)
                            tr(p6b[:, 0:128], VT[:, k0 + w: k0 + w + 127 * r + 1: r], IDB[:], [tbuf, tC], [tPB[6]])
                            cp(vcur, p6b[:, 0:128], [tPB[6]], [tvc], eng="act")
                            bs = u % 2
                            qs = QT[:, q0: q0 + 127 * r + 1: r]
                            mm(PB[bs][:, 0:128], KT[:, k0: k0 + 127 * r + 1: r], qs, True, True, [tbuf], [tPB[bs]])
                            mm(PB[bs][:, 128:256], KT[:, k0 + w: k0 + w + 127 * r + 1: r], qs, True, True, [tbuf],
                               [tPB[bs]])
                            act(PEX, PB[bs][:, 0:256], AF.Exp, [tPB[bs]], [tpex], scale=scale)
                            tt(PMK, PEX, MSK[:, 0 if n == 0 else 1, :], ALU.mult, [tpex, tC], [tpmk])
                            bo = 2 + (u % 2)
                            mm(PB[bo][:, 0:128], vprev, PMK[:, 0:128], True, False, [tvp, tpmk], [tPB[bo]])
                            mm(PB[bo][:, 0:128], vcur, PMK[:, 128:256], False, True, [tvc, tpmk], [tPB[bo]])
                            mm(PB[bo][:, 128:256], ONB[:], PMK[:, 0:128], True, False, [tC, tpmk], [tPB[bo]])
                            mm(PB[bo][:, 128:256], ONB[:], PMK[:, 128:256], False, True, [tC, tpmk], [tPB[bo]])
                            oa = OACC[:, q0: q0 + 127 * r + 1: r]
                            da = DACC[:, q0: q0 + 127 * r + 1: r]
                            if g == 0:
                                cp(oa, PB[bo][:, 0:128], [tPB[bo]], [tacc], eng="dve")
                                cp(da, PB[bo][:, 128:256], [tPB[bo]], [tacc], eng="dve")
                            else:
                                tt(oa, oa, PB[bo][:, 0:128], ALU.add, [tacc, tPB[bo]], [tacc])
                                tt(da, da, PB[bo][:, 128:256], ALU.add, [tacc, tPB[bo]], [tacc])
                            u += 1
                S.op("dve", lambda e: e.reciprocal(out=DACC, in_=DACC), [tacc], [tacc])
                tt(OBS, OACC, DACC, ALU.mult, [tacc], [tobs])
                st(obT[h * 128:(h + 1) * 128, :], OBS, [tobs], [tob])

        tob = Tok()

        def dump_R(t0):
            for b in range(4):
                osb = ATf[:, (b % 2) * 2048:((b % 2) + 1) * 2048]
                tos = tAT[(b % 2) * 8:((b % 2) + 1) * 8]
                for k4 in range(4):
                    bk = k4 % 4
                    for q in range(4):
                        kc = k4 * 4 + q
                        tr(PB[bk][:, q * 128:(q + 1) * 128], Rk(kc, b * 128, (b + 1) * 128), IDENT,
                           [tR[kc], tC], [tPB[bk]])
                    if k4 % 2 == 0:
                        cp(osb[:, k4 * 512:(k4 + 1) * 512], PB[bk][:, :], [tPB[bk]], tos, eng="act")
                    else:
                        cp(osb[:, k4 * 512:(k4 + 1) * 512], PB[bk][:, :], [tPB[bk]], tos, eng="dve")
                st(out_d[t0 + b * 128: t0 + (b + 1) * 128, :], osb, tos)

        def phaseC():
            for it in range(NT):
                t0 = it * T
                OA = AT[:, 0:8192]
                OB = AT[:, 8192:12288]
                SG = [AT[:, 12288 + i * 512: 12288 + (i + 1) * 512] for i in range(4)]
                tOA, tOB = tAT[0:16], tAT[16:24]
                tSG = tAT[24:28]
                ld(OA.rearrange("p (c t) -> p c t", c=16), oaT[:, t0:t0 + T].rearrange("(c p) t -> p c t", p=128),
                   tOA, reads=[toa])
                ld(OB.rearrange("p (c t) -> p c t", c=8), obT[:, t0:t0 + T].rearrange("(c p) t -> p c t", p=128),
                   tOB, reads=[tob])
                ld(R[:, 0:8192].rearrange("p (kc t) -> p kc t", kc=KC),
                   x1a[:, t0:t0 + T].rearrange("(kc p) t -> p kc t", p=128), list(tR))
                for nb in range(4):
                    sa = wload(wba_s[nb], KC, 512)
                    sb_ = wload(wbb_s[nb], 8, 512)
                    for c in range(4):
                        n = nb * 4 + c
                        ba, bb = (0, 1) if c % 2 == 0 else (2, 3)
                        for kc in range(KC):
                            mm(PB[ba][:, :], WRs(sa)[:, kc * 512 + c * 128: kc * 512 + (c + 1) * 128],
                               OA[:, kc * 512:(kc + 1) * 512], kc == 0, kc == KC - 1, [tWR[sa], tOA[kc]], [tPB[ba]])
                        for kc in range(8):
                            mm(PB[bb][:, :], WRs(sb_)[:, kc * 512 + c * 128: kc * 512 + (c + 1) * 128],
                               OB[:, kc * 512:(kc + 1) * 512], kc == 0, kc == 7, [tWR[sb_], tOB[kc]], [tPB[bb]])
                        i2 = (c % 2) * 2
                        ld(SG[i2], projT[(CGA + n) * 128:(CGA + n + 1) * 128, t0:t0 + T], [tSG[i2]], reads=[tproj])
                        ld(SG[i2 + 1], projT[(CGB + n) * 128:(CGB + n + 1) * 128, t0:t0 + T], [tSG[i2 + 1]],
                           reads=[tproj])
                        b = c % 2
                        tmp = TMPF[:, b * 512:(b + 1) * 512]
                        tsb = TSB[:, b * 512:(b + 1) * 512]
                        tt(tmp, PB[ba][:, :], SG[i2], ALU.mult, [tPB[ba], tSG[i2]], [tTMP[b]])
                        tt(tsb, PB[bb][:, :], SG[i2 + 1], ALU.mult, [tPB[bb], tSG[i2 + 1]], [tTSB[b]])
                        tt(Hk(n), tmp, tsb, ALU.add, [tTMP[b], tTSB[b]], [tH[n]])
                for nb in range(4):
                    so = wload(wout_s[nb], KC, 512)
                    for c in range(4):
                        n = nb * 4 + c
                        bk = 4 + (c % 2)
                        for kc in range(KC):
                            mm(PB[bk][:, :], WRs(so)[:, kc * 512 + c * 128: kc * 512 + (c + 1) * 128], Hk(kc),
                               kc == 0, kc == KC - 1, [tWR[so], tH[kc]], [tPB[bk]])
                        stt(Rk(n), PB[bk][:, :], pv(P_G2)[:, n:n + 1], Rk(n), ALU.mult, ALU.add,
                            [tPB[bk], tPV, tR[n]], [tR[n]])
                if KDUMP == 3:
                    dump_R(t0)
                    continue
                layernorm(P_L2RA, P_L2RB, P_L2HA, P_L2HB)
                ffn(1, P_HG3)
                layernorm(P_L3RA, P_L3RB, None, None)
                dump_R(t0)

        import os
        stage = int(os.environ.get("KSTAGE", "99"))
        phase0()
        S.barrier()
        if stage >= 1:
            phaseA()
            S.barrier()
        if stage >= 2:
            exchange_halo()
            gla_pass(False, state_only=True)
            exchange_state()
            S.barrier()
        if stage >= 3:
            gla_pass(True)
            S.barrier()
        if stage >= 4:
            attention()
            S.barrier()
        if stage >= 5:
            phaseC()
            S.barrier()
        emit_all(nc, S, es)
    return nc


_NC_CACHE = {}


def kernel(x, c, positions, w_ada, b_ada, ln1_g, ln1_b, w_ffn1_gu, w_ffn1_down, w_in, w_alpha2, b_alpha,
           gla_norm_g, w_branch_a, w_branch_b, w_out, ln2_g, ln2_b, w_ffn2_gu, w_ffn2_down, ln3_g, ln3_b):
    f = lambda a: np.ascontiguousarray(np.asarray(a, dtype=np.float32))
    x = f(x)
    c = f(c)
    positions = np.ascontiguousarray(np.asarray(positions, dtype=np.int32))
    lay = lambda v: np.ascontiguousarray(f(v).reshape(-1, 128).T)
    vecs = np.stack([lay(ln1_g[0]), lay(ln1_b[0]), lay(ln2_g[0]), lay(ln2_b[0]), lay(ln3_g[0]), lay(ln3_b[0]),
                     lay(f(gla_norm_g[0]).reshape(-1))], axis=1)
    wal = np.zeros((32, 1024), np.float32)
    wal[0:16] = f(w_alpha2[0])
    wal[16] = f(b_alpha[0])
    s_i = np.arange(128)[:, None]
    c_i = np.arange(128)[None, :]
    same = (s_i // 64) == (c_i // 64)
    consts = np.zeros((128, 8, 128), np.float32)
    consts[:, 0] = np.eye(128)
    consts[:, 1] = np.roll(np.eye(128), 64, axis=0)
    consts[:, 2] = 1.0 / D
    consts[:, 3] = np.where(same & (s_i > c_i), -1.0 / 16.0, 0.0)
    consts[:, 4] = np.where(same & (s_i <= c_i), -1.0 / 16.0, 0.0)
    consts[:, 5] = np.where(same & (s_i <= c_i), 1.0, 0.0)
    consts[:, 6, 0] = np.where(np.arange(128) < 64, -1.0 / 16.0, 0.0)
    consts[:, 6, 1] = np.where(np.arange(128) >= 64, -1.0 / 16.0, 0.0)
    half = 64
    freq = (10000.0 ** (-np.arange(half, dtype=np.float32) / half)).astype(np.float32)
    consts[:, 6, 2] = np.concatenate([freq, freq])
    consts[:, 6, 3] = np.where(np.arange(128) < 64, -1.0, 1.0)
    consts[:, 6, 5] = LN_EPS
    consts[:, 6, 6] = 0.0
    consts[:, 6, 7] = np.float32(math.pi / 2)
    j_i = np.arange(128)[:, None]
    q_i = np.arange(128)[None, :]
    mprev = (j_i >= q_i).astype(np.float32)
    mcur = (j_i <= q_i).astype(np.float32)
    shared = {
        "w_ada": f(w_ada[0]), "w_ffn1_gu": f(w_ffn1_gu[0]), "w_ffn2_gu": f(w_ffn2_gu[0]),
        "w_ffn1_down": f(w_ffn1_down[0]), "w_ffn2_down": f(w_ffn2_down[0]), "w_in": f(w_in[0]),
        "w_alpha_aug": wal, "w_branch_a": f(w_branch_a[0]), "w_branch_b": f(w_branch_b[0]), "w_out": f(w_out[0]),
        "vecs": np.ascontiguousarray(vecs), "b_ada_l": lay(b_ada[0]),
    }
    in_maps = []
    for core in range(8):
        b, hf = core // 2, core % 2
        cst = consts.copy()
        cst[:, 6, 4] = float(hf)
        masks = np.stack([np.concatenate([mprev * float(hf), mcur], 1), np.concatenate([mprev, mcur], 1)], 1)
        m = dict(shared)
        m["x"] = np.ascontiguousarray(x[b, hf * NTOK:(hf + 1) * NTOK])
        m["c_l"] = lay(c[b])
        m["pos"] = np.ascontiguousarray(positions[b, hf * NTOK:(hf + 1) * NTOK][None, :])
        m["consts"] = cst
        m["masks"] = np.ascontiguousarray(masks.astype(np.float32))
        in_maps.append(m)
    if "nc" not in _NC_CACHE:
        _NC_CACHE["nc"] = build_nc()
    import os
    ncores = int(os.environ.get("KCORES", "8"))
    res = run_bass_kernel_spmd(_NC_CACHE["nc"], in_maps[:ncores], core_ids=list(range(ncores)))
    out = np.zeros((BATCH, SEQ, D), np.float32)
    for core in range(ncores):
        b, hf = core // 2, core % 2
        out[b, hf * NTOK:(hf + 1) * NTOK] = res.results[core]["out"]
    return out
```
